# Optimizing a Trainium2 kernel written in Bass

```python
import math, functools
import jax, jax.numpy as jnp
from jax import lax
import numpy as np

D_MODEL = 1024
BATCH = 16
SEQ = 256
DEPTH = 2
DEC_BATCH = 8
DEC_SEQ = 4096
PAST_LEN = 256

GRID_W = 64
N_MIXERS = 2
N_LRU_LAYERS = (DEPTH + 1) // 2
N_S5_LAYERS = DEPTH // 2
N_DIR = 2
D_RNN = D_MODEL
LRU_HEADS = 16
LRU_BLOCK = D_RNN // LRU_HEADS
LRU_C = 8.0
LRU_CONV = 4
S5_GROUP = 16
S5_GROUPS = D_MODEL // S5_GROUP
S5_STATE = 64
D_FF = 2816
FFN_CONV = 3
N_MOD = 6
EPS = 1e-6

kernel_name = "hybrid_rglru_s5_diffusion_step"

F32 = jnp.float32


def rms_norm(x, g):
    x32 = x.astype(F32)
    y = x32 * lax.rsqrt(jnp.mean(x32 * x32, axis=-1, keepdims=True) + EPS)
    return (y * g.astype(F32)).astype(x.dtype)


def modulate(x, g, shift, scale):
    return rms_norm(x, g) * (1 + scale) + shift


def depthwise_conv_seq(x, w, b, pad_left, pad_right):
    L = x.shape[1]
    xp = jnp.pad(x, ((0, 0), (pad_left, pad_right), (0, 0)))
    y = b
    for k in range(w.shape[0]):
        y = y + xp[:, k:k + L] * w[k]
    return y


def _combine(e1, e2):
    a1, b1 = e1
    a2, b2 = e2
    return a1 * a2, a2 * b1 + b2


def linear_scan(a, b, h0, reverse):
    edge = -1 if reverse else 0
    b = b.at[:, edge].add(a[:, edge] * h0)
    _, h = lax.associative_scan(_combine, (a, b), reverse=reverse, axis=1)
    return h, (h[:, 0] if reverse else h[:, -1])


def rglru_mixer(h, h0, w_in, conv_w, conv_b, w_a, b_a, w_i, b_i, lam, w_out):
    B, L, _ = h.shape
    gate, xr = jnp.split(h @ w_in, 2, axis=-1)
    xr = depthwise_conv_seq(xr, conv_w, conv_b, 2, 1).astype(F32)
    xb = xr.reshape(B, L, LRU_HEADS, LRU_BLOCK)
    y = 0.0
    finals = []
    for d in range(N_DIR):
        r = jax.nn.sigmoid(jnp.einsum('blhi,hij->blhj', xb, w_a[d].astype(F32)).reshape(B, L, D_RNN) + b_a[d].astype(F32))
        i = jax.nn.sigmoid(jnp.einsum('blhi,hij->blhj', xb, w_i[d].astype(F32)).reshape(B, L, D_RNN) + b_i[d].astype(F32))
        log_a = -LRU_C * r * jax.nn.softplus(-lam[d].astype(F32))
        a = jnp.exp(log_a)
        bx = jnp.sqrt(-jnp.expm1(2.0 * log_a)) * (i * xr)
        hs, hf = linear_scan(a, bx, h0[:, d], reverse=(d == 1))
        y = y + hs
        finals.append(hf)
    out = (y * jax.nn.gelu(gate.astype(F32))).astype(h.dtype) @ w_out
    return out, jnp.stack(finals, axis=1)


def s5_mixer(h, h0, a_re, a_im, log_dt, b_re, b_im, c_re, c_im, d_skip, w_glu):
    B, L, _ = h.shape
    u = h.astype(F32).reshape(B, L, S5_GROUPS, S5_GROUP)
    uc = u.astype(jnp.complex64)
    y = d_skip.astype(F32).reshape(S5_GROUPS, S5_GROUP) * u
    finals = []
    for d in range(N_DIR):
        lam = lax.complex(a_re[d].astype(F32), a_im[d].astype(F32))
        dt = jnp.exp(log_dt[d].astype(F32))[:, None]
        a_bar = jnp.exp(lam * dt)
        b_bar = lax.complex(b_re[d].astype(F32), b_im[d].astype(F32)) * ((a_bar - 1.0) / lam)[..., None]
        bu = jnp.einsum('blgc,gpc->blgp', uc, b_bar)
        a_seq = jnp.broadcast_to(a_bar, (1, L) + a_bar.shape)
        hs, hf = linear_scan(a_seq, bu, h0[:, d], reverse=(d == 1))
        cc = lax.complex(c_re[d].astype(F32), c_im[d].astype(F32))
        y = y + jnp.real(jnp.einsum('blgp,gcp->blgc', hs, cc))
        finals.append(hf)
    z = jax.nn.gelu(y.reshape(B, L, D_MODEL)).astype(h.dtype)
    v, g = jnp.split(z @ w_glu, 2, axis=-1)
    return v * jax.nn.sigmoid(g), jnp.stack(finals, axis=1)


def conv_ffn(h, on_grid, w_up, conv_w, conv_b, w_down):
    B, L, _ = h.shape
    up = h @ w_up
    if on_grid:
        rows = L // GRID_W
        up = depthwise_conv_seq(up.reshape(B * rows, GRID_W, 2 * D_FF), conv_w, conv_b, 1, 1).reshape(B, L, 2 * D_FF)
    else:
        up = depthwise_conv_seq(up, conv_w, conv_b, 1, 1)
    v, g = jnp.split(up, 2, axis=-1)
    return (v * jax.nn.silu(g)) @ w_down


def apply_layer(x, mod, h0, on_grid, mixer, g_mix, g_ffn, w_up, conv_w, conv_b, w_down):
    sh1, sc1, gt1, sh2, sc2, gt2 = jnp.split(mod, N_MOD, axis=-1)
    out, final = mixer(modulate(x, g_mix, sh1, sc1), h0)
    x = x + gt1 * out
    x = x + gt2 * conv_ffn(modulate(x, g_ffn, sh2, sc2), on_grid, w_up, conv_w, conv_b, w_down)
    return x, final


def setup_inputs(seed: int = 0) -> dict:
    key = jax.random.key(seed)
    ks = iter(jax.random.split(key, 40))

    def nrm(shape, scale):
        return jax.random.normal(next(ks), shape, F32) * scale

    u = jax.random.uniform(next(ks), (N_LRU_LAYERS, N_DIR, D_RNN), F32, minval=0.9, maxval=0.999)
    lru_lambda = jnp.log(u) - jnp.log1p(-u)
    a_im = jnp.pi * jnp.arange(S5_STATE, dtype=F32)
    return {
        "x_prompt": nrm((BATCH, SEQ, D_MODEL), 1.0),
        "x_sample": nrm((DEC_BATCH, DEC_SEQ, D_MODEL), 1.0),
        "state_lru": nrm((DEC_BATCH, N_LRU_LAYERS, N_DIR, D_RNN), 0.5),
        "state_s5_re": nrm((DEC_BATCH, N_S5_LAYERS, N_DIR, S5_GROUPS, S5_STATE), 0.5),
        "state_s5_im": nrm((DEC_BATCH, N_S5_LAYERS, N_DIR, S5_GROUPS, S5_STATE), 0.5),
        "c": nrm((DEC_BATCH, D_MODEL), 1.0),
        "c_ctx": nrm((D_MODEL,), 1.0),
        "ada_w": nrm((DEPTH, D_MODEL, N_MOD * D_MODEL), 0.5 * D_MODEL ** -0.5),
        "ada_b": nrm((DEPTH, N_MOD * D_MODEL), 0.02),
        "norm_mix": 1.0 + nrm((DEPTH, D_MODEL), 0.02),
        "norm_ffn": 1.0 + nrm((DEPTH, D_MODEL), 0.02),
        "norm_final": 1.0 + nrm((D_MODEL,), 0.02),
        "lru_w_in": nrm((N_LRU_LAYERS, D_MODEL, 2 * D_RNN), D_MODEL ** -0.5),
        "lru_conv_w": nrm((N_LRU_LAYERS, LRU_CONV, D_RNN), 0.5),
        "lru_conv_b": nrm((N_LRU_LAYERS, D_RNN), 0.02),
        "lru_w_a": nrm((N_LRU_LAYERS, N_DIR, LRU_HEADS, LRU_BLOCK, LRU_BLOCK), LRU_BLOCK ** -0.5),
        "lru_b_a": nrm((N_LRU_LAYERS, N_DIR, D_RNN), 0.02),
        "lru_w_i": nrm((N_LRU_LAYERS, N_DIR, LRU_HEADS, LRU_BLOCK, LRU_BLOCK), LRU_BLOCK ** -0.5),
        "lru_b_i": nrm((N_LRU_LAYERS, N_DIR, D_RNN), 0.02),
        "lru_lambda": lru_lambda,
        "lru_w_out": nrm((N_LRU_LAYERS, D_RNN, D_MODEL), D_RNN ** -0.5),
        "s5_a_re": -0.5 + nrm((N_S5_LAYERS, N_DIR, S5_GROUPS, S5_STATE), 0.01),
        "s5_a_im": a_im + nrm((N_S5_LAYERS, N_DIR, S5_GROUPS, S5_STATE), 0.01),
        "s5_log_dt": jax.random.uniform(next(ks), (N_S5_LAYERS, N_DIR, S5_GROUPS), F32, minval=math.log(0.001), maxval=math.log(0.1)),
        "s5_b_re": nrm((N_S5_LAYERS, N_DIR, S5_GROUPS, S5_STATE, S5_GROUP), (2 * S5_GROUP) ** -0.5),
        "s5_b_im": nrm((N_S5_LAYERS, N_DIR, S5_GROUPS, S5_STATE, S5_GROUP), (2 * S5_GROUP) ** -0.5),
        "s5_c_re": nrm((N_S5_LAYERS, N_DIR, S5_GROUPS, S5_GROUP, S5_STATE), (2 * S5_STATE) ** -0.5),
        "s5_c_im": nrm((N_S5_LAYERS, N_DIR, S5_GROUPS, S5_GROUP, S5_STATE), (2 * S5_STATE) ** -0.5),
        "s5_d": nrm((N_S5_LAYERS, D_MODEL), 0.5),
        "s5_w_glu": nrm((N_S5_LAYERS, D_MODEL, 2 * D_MODEL), D_MODEL ** -0.5),
        "ffn_w_up": nrm((DEPTH, D_MODEL, 2 * D_FF), D_MODEL ** -0.5),
        "ffn_conv_w": nrm((DEPTH, FFN_CONV, 2 * D_FF), 0.5),
        "ffn_conv_b": nrm((DEPTH, 2 * D_FF), 0.02),
        "ffn_w_down": nrm((DEPTH, D_FF, D_MODEL), D_FF ** -0.5),
    }


def reference(x_prompt, x_sample, state_lru, state_s5_re, state_s5_im, c, c_ctx,
              ada_w, ada_b, norm_mix, norm_ffn, norm_final,
              lru_w_in, lru_conv_w, lru_conv_b, lru_w_a, lru_b_a, lru_w_i, lru_b_i, lru_lambda, lru_w_out,
              s5_a_re, s5_a_im, s5_log_dt, s5_b_re, s5_b_im, s5_c_re, s5_c_im, s5_d, s5_w_glu,
              ffn_w_up, ffn_conv_w, ffn_conv_b, ffn_w_down):
    x_p, x_s = x_prompt, x_sample
    bp = x_p.shape[0]
    new_lru, new_s5 = [], []
    for l in range(DEPTH):
        j = l // N_MIXERS
        mod_p = (jax.nn.silu(c_ctx) @ ada_w[l] + ada_b[l])[None, None, :]
        mod_s = (jax.nn.silu(c) @ ada_w[l] + ada_b[l])[:, None, :]
        if l % N_MIXERS == 0:
            mixer = functools.partial(rglru_mixer, w_in=lru_w_in[j], conv_w=lru_conv_w[j], conv_b=lru_conv_b[j],
                                      w_a=lru_w_a[j], b_a=lru_b_a[j], w_i=lru_w_i[j], b_i=lru_b_i[j],
                                      lam=lru_lambda[j], w_out=lru_w_out[j])
            h0_p = jnp.zeros((bp, N_DIR, D_RNN), F32)
            h0_s = state_lru[:, j].astype(F32)
        else:
            mixer = functools.partial(s5_mixer, a_re=s5_a_re[j], a_im=s5_a_im[j], log_dt=s5_log_dt[j],
                                      b_re=s5_b_re[j], b_im=s5_b_im[j], c_re=s5_c_re[j], c_im=s5_c_im[j],
                                      d_skip=s5_d[j], w_glu=s5_w_glu[j])
            h0_p = jnp.zeros((bp, N_DIR, S5_GROUPS, S5_STATE), jnp.complex64)
            h0_s = lax.complex(state_s5_re[:, j].astype(F32), state_s5_im[:, j].astype(F32))
        x_p, fin_p = apply_layer(x_p, mod_p, h0_p, False, mixer, norm_mix[l], norm_ffn[l],
                                 ffn_w_up[l], ffn_conv_w[l], ffn_conv_b[l], ffn_w_down[l])
        x_s, _ = apply_layer(x_s, mod_s, h0_s, True, mixer, norm_mix[l], norm_ffn[l],
                             ffn_w_up[l], ffn_conv_w[l], ffn_conv_b[l], ffn_w_down[l])
        if l % N_MIXERS == 0:
            new_lru.append(fin_p)
        else:
            new_s5.append(fin_p)
    y_prompt = rms_norm(x_p, norm_final)
    y_sample = rms_norm(x_s, norm_final)
    new_state_lru = jnp.stack(new_lru, axis=1)
    s5_state = jnp.stack(new_s5, axis=1)
    return (y_prompt, y_sample, new_state_lru, jnp.real(s5_state), jnp.imag(s5_state))
```

```python
import contextlib
import numpy as np
import concourse.bass as bass
import concourse.mybir as mybir
from concourse.bass_utils import run_bass_kernel_spmd

F32 = mybir.dt.float32
BF16 = mybir.dt.bfloat16
AF = mybir.ActivationFunctionType
ALU = mybir.AluOpType
AX = mybir.AxisListType

D = 1024
NT = 4608
LS = 4096
LP = 256
DFF = 2816
NF = 22
EPS = 1e-6
NCH = 576
SEQS = [(0, 4096), (4096, 256), (4352, 256)]
TT = [(0, 128), (1024, 128), (2048, 128), (3072, 128), (4096, 64)]
ENGS = ['sync', 'scalar', 'vector', 'gpsimd', 'tensor']


class Sched:
    def __init__(self, nc, stack):
        self.nc = nc
        self.stack = stack
        self.q = {e: [] for e in ENGS}
        self.cnt = {e: 0 for e in ENGS}
        self.waited = {e: {} for e in ENGS}
        self.lastw = {}
        self.readers = {}
        self.sems = {}
        self.dcount = {}
        self.pending = {e: {} for e in ENGS}
        for e in ENGS:
            self.sem('E' + e)

    def sem(self, key):
        if key not in self.sems:
            self.sems[key] = self.stack.enter_context(self.nc.semaphore('s' + str(len(self.sems))))
            self.dcount[key] = 0
        return self.sems[key]

    def op(self, eng, fn, r=(), w=(), dsem=None, skip_self=False):
        deps = dict(self.pending[eng])
        self.pending[eng] = {}

        def add(tok):
            k, v, te = tok
            if te == 'tensor' and eng == 'tensor':
                return
            if skip_self and te == eng:
                return
            if deps.get(k, 0) < v:
                deps[k] = v
        for x in r:
            if x in self.lastw:
                add(self.lastw[x])
        for x in w:
            if x in self.lastw:
                add(self.lastw[x])
            for t in self.readers.get(x, ()):
                add(t)
        waits = []
        for k, v in deps.items():
            if self.waited[eng].get(k, 0) < v:
                self.waited[eng][k] = v
                waits.append((k, v))
        if dsem is None:
            self.cnt[eng] += 1
            tok = ('E' + eng, self.cnt[eng], eng)
            inc = 1
        else:
            self.sem(dsem)
            self.dcount[dsem] += 16
            tok = (dsem, self.dcount[dsem], 'dma')
            inc = 16
        self.q[eng].append((waits, fn, tok[0], inc))
        for x in r:
            self.readers.setdefault(x, []).append(tok)
        for x in w:
            self.lastw[x] = tok
            self.readers[x] = []
        return tok

    def barrier(self):
        allt = {}
        for e in ENGS:
            if self.cnt[e]:
                allt['E' + e] = self.cnt[e]
        for k, v in self.dcount.items():
            if not k.startswith('E') and v:
                allt[k] = v
        for e in ENGS:
            for k, v in allt.items():
                if self.pending[e].get(k, 0) < v:
                    self.pending[e][k] = v
        self.lastw = {}
        self.readers = {}

    def emit(self, block):
        nc = self.nc

        def mk(ename):
            def body(e):
                for waits, fn, semk, inc in self.q[ename]:
                    for k, v in waits:
                        e.wait_ge(self.sems[k], v)
                    fn(e).then_inc(self.sems[semk], inc)
                for k, v in self.final.items():
                    e.wait_ge(self.sems[k], v)
            return body
        self.final = {}
        for e in ENGS:
            if self.cnt[e]:
                self.final['E' + e] = self.cnt[e]
        for k, v in self.dcount.items():
            if not k.startswith('E') and v:
                self.final[k] = v
        block.sync(mk('sync'))
        block.scalar(mk('scalar'))
        block.vector(mk('vector'))
        block.gpsimd(mk('gpsimd'))
        block.tensor(mk('tensor'))


def build(stop_after='E'):
    nc = bass.Bass("TRN2", target_bir_lowering=False)
    I = {}

    def din(name, shape, dt=F32):
        I[name] = nc.dram_tensor(name, list(shape), dt, kind="ExternalInput").ap()
        return I[name]

    def dout(name, shape, dt=F32):
        return nc.dram_tensor(name, list(shape), dt, kind="ExternalOutput").ap()

    def dscr(name, shape, dt):
        return nc.dram_tensor(name, list(shape), dt, kind="Internal").ap()

    xs = din("xs", [LS, D]); xp = din("xp", [2 * LP, D])
    st_lru = din("st_lru", [2, D]); st_re = din("st_re", [2, 64, 64]); st_im = din("st_im", [2, 64, 64])
    cvec = din("cvec", [2, D])
    ada_w = din("ada_w", [2, D, 6 * D]); ada_b = din("ada_b", [2, 6 * D])
    norm_mix = din("norm_mix", [2, D]); norm_ffn = din("norm_ffn", [2, D]); norm_final = din("norm_final", [1, D])
    lru_w_in = din("lru_w_in", [D, 2 * D]); lru_conv_w = din("lru_conv_w", [4, D]); lru_conv_b = din("lru_conv_b", [1, D])
    lru_w_a = din("lru_w_a", [2, 16, 64, 64]); lru_b_a = din("lru_b_a", [2, D])
    lru_w_i = din("lru_w_i", [2, 16, 64, 64]); lru_b_i = din("lru_b_i", [2, D])
    lru_lambda = din("lru_lambda", [2, D]); lru_w_out = din("lru_w_out", [D, D])
    s5_a_re = din("s5_a_re", [2, 64, 64]); s5_a_im = din("s5_a_im", [2, 64, 64]); s5_log_dt = din("s5_log_dt", [2, 64])
    s5_b_re = din("s5_b_re", [2, 64, 64, 16]); s5_b_im = din("s5_b_im", [2, 64, 64, 16])
    s5_c_re = din("s5_c_re", [2, 64, 16, 64]); s5_c_im = din("s5_c_im", [2, 64, 16, 64])
    s5_d = din("s5_d", [1, D]); s5_w_glu = din("s5_w_glu", [D, 2 * D])
    ffn_w_up = din("ffn_w_up", [2, D, 2 * DFF]); ffn_conv_w = din("ffn_conv_w", [2, 3, 2 * DFF])
    ffn_conv_b = din("ffn_conv_b", [2, 2 * DFF]); ffn_w_down = din("ffn_w_down", [2, DFF, D])

    ys = dout("ys", [LS, D]); yp = dout("yp", [2 * LP, D])
    nlru = dout("nlru", [2, 2, D]); ns5re = dout("ns5re", [2, 2, 64, 64]); ns5im = dout("ns5im", [2, 2, 64, 64])

    MODX = dscr("MODX", [2, 2, 6, D], F32)
    Z0 = dscr("Z0", [8, 128, NT], BF16)
    WUP = dscr("WUP", [2, NF, 128, 8, 2, 128], BF16)
    UG = dscr("UG", [64, 128, NCH], BF16)
    HSD = dscr("HSD", [2, 128, 64, NCH + 3], BF16)
    dbg = {}
    if stop_after != 'E':
        dbg['xs1'] = dout("dbg_xs", [LS, D]); dbg['xp1'] = dout("dbg_xp", [2 * LP, D])

    stack = contextlib.ExitStack()
    with stack:
        NW = 52200
        big = stack.enter_context(nc.sbuf_tensor("big", [128, NW], F32))
        ps = stack.enter_context(nc.psum_tensor("ps", [128, 8, 512], F32))
        P = Sched(nc, stack)
        stack.enter_context(nc.allow_non_contiguous_dma(reason="small strided parameter loads"))

        class Bump:
            def __init__(s, base, end):
                s.base = base; s.cur = base; s.end = end

            def f32(s, n):
                a = s.cur; s.cur += n
                assert s.cur <= s.end, (s.cur, s.end)
                return big[:, a:a + n]

            def bf(s, n):
                w = (n + 1) // 2
                a = s.cur; s.cur += w
                assert s.cur <= s.end, (s.cur, s.end)
                return big[:, a:a + w].bitcast(BF16)

        def psb(b, nb=1):
            return ps[:, b:b + nb, :].rearrange("p b n -> p (b n)") if nb > 1 else ps[:, b, :]

        def psbf(b):
            return ps[:, b, :].bitcast(BF16)

        def dump(name, src, shape, dt=F32, r=()):
            o = dout("dbg_" + name, shape, dt)
            P.op('sync', lambda e: e.dma_start(out=o, in_=src), r=list(r), dsem='stdbg')

        def finish():
            with nc.Block() as block:
                P.emit(block)
            return nc

        pers = Bump(0, 1400)
        ident = pers.bf(128)
        identf = pers.f32(128)
        modcol = pers.f32(2 * 2 * 4 * 8).rearrange("p (l g k c) -> p l g k c", l=2, g=2, k=4)
        lconv = pers.f32(40).rearrange("p (t c) -> p t c", t=5)
        lba = pers.f32(32).rearrange("p (w d c) -> p w d c", w=2, d=2)
        lcl = pers.f32(32).rearrange("p (k d c) -> p k d c", k=2, d=2)
        fconv = pers.f32(2 * 4 * 44).rearrange("p (l t f) -> p l t f", l=2, t=4)
        lruout = pers.f32(32).rearrange("p (s d c) -> p s d c", s=2, d=2)
        small = pers.f32(64)
        PBASE = 1400

        P.op('gpsimd', lambda e: e.memset(identf, 0.0), w=['identf'])
        P.op('gpsimd', lambda e: e.affine_select(out=identf, in_=identf, pattern=[[-1, 128]], compare_op=ALU.not_equal,
                                                 fill=1.0, base=0, channel_multiplier=1), r=['identf'], w=['identf'])
        P.op('gpsimd', lambda e: e.tensor_copy(out=ident, in_=identf), r=['identf'], w=['ident'])

        wupconv = []
        for l in range(2):
            wv = ffn_w_up[l].rearrange("(ci p) (h f m) -> f p ci h m", p=128, h=2, m=128)
            for f in range(NF):
                for h in range(2):
                    wupconv.append((l, f, h, wv))

        def emit_wupconv(n):
            for _ in range(n):
                if not wupconv:
                    return
                l, f, h, wv = wupconv.pop(0)
                P.op('gpsimd', lambda e, l=l, f=f, h=h, wv=wv: e.dma_start(out=WUP[l, f, :, :, h, :], in_=wv[f, :, :, h, :]),
                     w=['WUP%d_%d_%d' % (l, f, h)], dsem='wupcv')

        def ldcol(dst, src, res, eng='sync'):
            P.op(eng, lambda e: e.dma_start(out=dst, in_=src), w=[res], dsem='ld' + res)
        for t in range(4):
            ldcol(lconv[:, t, :], lru_conv_w[t].rearrange("(c p) -> p c", p=128), 'lconv')
        ldcol(lconv[:, 4, :], lru_conv_b[0].rearrange("(c p) -> p c", p=128), 'lconv')
        for d in range(2):
            ldcol(lba[:, 0, d, :], lru_b_a[d].rearrange("(c p) -> p c", p=128), 'lba')
            ldcol(lba[:, 1, d, :], lru_b_i[d].rearrange("(c p) -> p c", p=128), 'lba')
            ldcol(lcl[:, 0, d, :], lru_lambda[d].rearrange("(c p) -> p c", p=128), 'lcl')
            ldcol(lcl[:, 1, d, :], st_lru[d].rearrange("(c p) -> p c", p=128), 'lcl')
        for l in range(2):
            for t in range(3):
                ldcol(fconv[:, l, t, :], ffn_conv_w[l, t].rearrange("(f p) -> p f", p=128), 'fconv')
            ldcol(fconv[:, l, 3, :], ffn_conv_b[l].rearrange("(f p) -> p f", p=128), 'fconv')
        clv = lcl[:, 0, :, :]
        P.op('scalar', lambda e: e.activation(out=clv, in_=clv, func=AF.Exp, scale=-1.0), r=['lcl'], w=['lcl'])
        P.op('scalar', lambda e: e.activation(out=clv, in_=clv, func=AF.Ln, bias=1.0, scale=1.0), r=['lcl'], w=['lcl'])
        P.op('vector', lambda e: e.tensor_scalar(out=clv, in0=clv, scalar1=-8.0, scalar2=None, op0=ALU.mult), r=['lcl'], w=['lcl'])

        sb = Bump(PBASE, NW)
        cvT = sb.f32(16).rearrange("p (c g) -> p c g", g=2)
        adaw = [sb.f32(8 * 512).rearrange("p (c n) -> p c n", c=8) for _ in range(2)]
        modrow = sb.f32(6 * D)
        nrow = sb.f32(3 * D)
        for g in range(2):
            P.op('sync', lambda e, g=g: e.dma_start(out=cvT[:, :, g], in_=cvec[g].rearrange("(c p) -> p c", p=128)), w=['cvT'], dsem='ldcvT')
        cvf = cvT.rearrange("p c g -> p (c g)")
        P.op('scalar', lambda e: e.activation(out=cvf, in_=cvf, func=AF.Silu), r=['cvT'], w=['cvT'])
        for l in range(2):
            P.op('sync', lambda e, l=l: e.dma_start(out=modrow[0:2, :], in_=ada_b[l:l + 1, :].broadcast_to([2, 6 * D])), w=['modrow'], dsem='ldmr0')
            P.op('sync', lambda e, l=l: e.dma_start(out=nrow[0:2, 0:D], in_=norm_mix[l:l + 1, :].broadcast_to([2, D])), w=['nrow'], dsem='ldmr1')
            P.op('sync', lambda e, l=l: e.dma_start(out=nrow[0:2, D:2 * D], in_=norm_ffn[l:l + 1, :].broadcast_to([2, D])), w=['nrow'], dsem='ldmr1')
            for pc in range(12):
                buf = adaw[pc % 2]
                P.op('sync', lambda e, l=l, pc=pc, buf=buf: e.dma_start(
                    out=buf, in_=ada_w[l, :, pc * 512:(pc + 1) * 512].rearrange("(c p) n -> p c n", p=128)),
                    w=['adaw%d' % (pc % 2)], dsem='adaw%d' % (pc % 2))
                bk = pc % 2
                for ci in range(8):
                    P.op('tensor', lambda e, ci=ci, buf=buf, bk=bk: e.matmul(ps[0:2, bk, :], lhsT=cvT[:, ci, :], rhs=buf[:, ci, :],
                                                                           start=(ci == 0), stop=(ci == 7)),
                         r=['cvT', 'adaw%d' % (pc % 2)], w=['ps%d' % bk])
                mr = modrow[0:2, pc * 512:(pc + 1) * 512]
                P.op('vector', lambda e, mr=mr, bk=bk: e.tensor_tensor(out=mr, in0=mr, in1=ps[0:2, bk, :], op=ALU.add),
                     r=['ps%d' % bk, 'modrow'], w=['modrow'])
            def mrc(i):
                return modrow[0:2, i * D:(i + 1) * D]
            P.op('vector', lambda e: e.scalar_tensor_tensor(out=mrc(1), in0=mrc(1), scalar=1.0, in1=nrow[0:2, 0:D], op0=ALU.add, op1=ALU.mult),
                 r=['modrow', 'nrow'], w=['modrow'])
            P.op('vector', lambda e: e.scalar_tensor_tensor(out=mrc(4), in0=mrc(4), scalar=1.0, in1=nrow[0:2, D:2 * D], op0=ALU.add, op1=ALU.mult),
                 r=['modrow', 'nrow'], w=['modrow'])
            for kind, ch in enumerate([2, 5, 1, 0, 4, 3]):
                P.op('sync', lambda e, l=l, kind=kind, ch=ch: e.dma_start(out=MODX[l, :, kind, :], in_=modrow[0:2, ch * D:(ch + 1) * D]),
                     r=['modrow'], w=['MODX'], dsem='stmodx')
        for l in range(2):
            for g in range(2):
                for k in range(4):
                    P.op('sync', lambda e, l=l, g=g, k=k: e.dma_start(out=modcol[:, l, g, k, :], in_=MODX[l, g, 2 + k].rearrange("(c p) -> p c", p=128)),
                         r=['MODX'], w=['modcol'], dsem='ldsmall')
        P.barrier()

        def load_bc(dst, l, g, kind, np_, res):
            P.op('sync', lambda e: e.dma_start(out=dst[0:np_, :], in_=MODX[l, g, kind:kind + 1, :].broadcast_to([np_, D])),
                 r=['MODX'], w=[res], dsem='ld' + res)

        def xtile_dram(t, for_out=False, scr=False):
            base, np_ = TT[t]
            if t < 4:
                src = (ys if (for_out or scr) else xs)[base:base + 1024, :]
            else:
                src = (yp if (for_out or scr) else xp)[:, :]
            return src.rearrange("(k j) c -> k j c", j=8)

        def rms_rstd(XT, np_, rstd, junk, tag, jres=None):
            ssq = small[:, 0:8]
            for j in range(8):
                P.op('scalar', lambda e, j=j: e.activation(out=junk[0:np_, j, :], in_=XT[0:np_, j, :], func=AF.Square, accum_out=ssq[0:np_, j:j + 1]),
                     r=[tag + 'XT%d' % j], w=['ssq', (jres(j) if jres else tag + 'XN%d' % j)])
            P.op('vector', lambda e: e.tensor_scalar(out=rstd[0:np_, :], in0=ssq[0:np_, :], scalar1=1.0 / D, scalar2=EPS, op0=ALU.mult, op1=ALU.add),
                 r=['ssq'], w=['rstd'])
            P.op('scalar', lambda e: e.activation(out=rstd[0:np_, :], in_=rstd[0:np_, :], func=AF.Sqrt), r=['rstd'], w=['rstd'])
            P.op('vector', lambda e: e.reciprocal(out=rstd[0:np_, :], in_=rstd[0:np_, :]), r=['rstd'], w=['rstd'])

        bankrr = [0]

        def nextbank(n=1):
            b = bankrr[0]
            if b + n > 8:
                b = 0
            bankrr[0] = (b + n) % 8
            return b

        def to_fm(SRC, np_, dst_fn, rtag, wtag, scale_col=None, bias_col=None):
            for ci in range(8):
                b = nextbank()
                pv = psbf(b)[:, 0:8 * np_].rearrange("p (j k) -> p j k", j=8)
                for j in range(8):
                    P.op('tensor', lambda e, ci=ci, j=j, pv=pv: e.transpose(out=pv[:, j, :], in_=SRC[0:np_, j, ci * 128:(ci + 1) * 128],
                                                                            identity=ident[0:np_, 0:np_]),
                         r=[rtag + '%d' % j, 'ident'], w=['ps%d' % b])
                dst = dst_fn(ci).rearrange("p (k j) -> p j k", j=8)
                if scale_col is not None:
                    if ci % 2 == 0:
                        P.op('scalar', lambda e, ci=ci, pv=pv, dst=dst: e.activation(out=dst, in_=pv, func=AF.Identity,
                                                                                   scale=scale_col[:, ci:ci + 1], bias=bias_col[:, ci:ci + 1]),
                             r=['ps%d' % b, 'modcol'], w=[wtag + '%d' % ci])
                    else:
                        P.op('vector', lambda e, ci=ci, pv=pv, dst=dst: e.tensor_scalar(out=dst, in0=pv, scalar1=scale_col[:, ci:ci + 1],
                                                                                      scalar2=bias_col[:, ci:ci + 1], op0=ALU.mult, op1=ALU.add),
                             r=['ps%d' % b, 'modcol'], w=[wtag + '%d' % ci])
                else:
                    if ci % 2 == 0:
                        P.op('scalar', lambda e, pv=pv, dst=dst: e.activation(out=dst, in_=pv, func=AF.Copy), r=['ps%d' % b], w=[wtag + '%d' % ci])
                    else:
                        P.op('vector', lambda e, pv=pv, dst=dst: e.tensor_copy(out=dst, in_=pv), r=['ps%d' % b], w=[wtag + '%d' % ci])

        ab = Bump(PBASE, NW)
        Hb = ab.bf(8 * NT).rearrange("p (c n) -> p c n", c=8)
        phB_base = ab.cur
        XTa = ab.f32(8 * D).rearrange("p (j c) -> p j c", j=8)
        XNa = ab.bf(8 * D).rearrange("p (j c) -> p j c", j=8)
        rstd = small[:, 8:16]
        for t in range(5):
            base, np_ = TT[t]
            g = 0 if t < 4 else 1
            P.op('sync', lambda e, t=t, np_=np_: e.dma_start(out=XTa[0:np_], in_=xtile_dram(t)),
                 w=['aXT%d' % j for j in range(8)], dsem='ldXTa')
            rms_rstd(XTa, np_, rstd, XNa, 'a')
            for j in range(8):
                eng = 'vector' if j % 2 == 0 else 'gpsimd'
                P.op(eng, lambda e, j=j, np_=np_: e.tensor_scalar(out=XNa[0:np_, j, :], in0=XTa[0:np_, j, :], scalar1=rstd[0:np_, j:j + 1],
                                                                  scalar2=None, op0=ALU.mult),
                     r=['aXT%d' % j, 'rstd', 'aXN%d' % j], w=['aXN%d' % j])
            to_fm(XNa, np_, lambda ci, base=base, np_=np_: Hb[:, ci, base:base + 8 * np_], 'aXN', 'H_t%d_' % t,
                  scale_col=modcol[:, 0, g, 0, :], bias_col=modcol[:, 0, g, 1, :])
        Hres = []
        P.barrier()
        if stop_after == 'C1':
            dump('H_A', Hb, [128, 8, NT], BF16)
        if stop_after == 'A':
            dump('modx', MODX, [2, 2, 6, D])
            dump('modcol', modcol.rearrange("p l g k c -> p (l g k c)"), [128, 128])
            dump('lcl', lcl.rearrange("p k d c -> p (k d c)"), [128, 32])
            dump('H', Hb, [128, 8, NT], BF16)
            return finish()

        bb = Bump(phB_base, NW)
        WIN = [bb.bf(8 * 2 * 128).rearrange("p (c h m) -> p c h m", c=8, h=2) for _ in range(2)]
        WAI = bb.bf(8 * 2 * 2 * 128).rearrange("p (c w d m) -> p c w d m", c=8, w=2, d=2)
        WAIf = bb.f32(2 * 2 * 128).rearrange("p (w d m) -> p w d m", w=2, d=2)
        GG = bb.bf(NT)
        XC = bb.f32(NT)
        XCb = bb.bf(NT)
        RA = bb.f32(NT)
        IG = bb.f32(NT)
        T01 = [bb.f32(NT), bb.f32(NT)]
        NTL = [(i * 512, 512) for i in range(9)]

        ESEG = [(0, 2047), (2047, 4096), (4096, 4608)]

        def segs_of(lo, hi):
            return [i for i, (a_, b_) in enumerate(ESEG) if a_ < hi and b_ > lo]

        def rs(name, lo, hi):
            return ['%s_%d' % (name, i) for i in segs_of(lo, hi)]

        for c in range(8):
            wb = WIN[c % 2]
            for h in range(2):
                P.op('gpsimd', lambda e, c=c, h=h, wb=wb: e.dma_start(
                    out=wb[:, :, h, :], in_=lru_w_in[:, h * D + c * 128: h * D + (c + 1) * 128].rearrange("(ci p) m -> p ci m", p=128)),
                    w=['WIN%d' % (c % 2)], dsem='ldWIN%d' % (c % 2))
            P.op('gpsimd', lambda e: e.memset(WAIf.rearrange("p w d m -> p (w d m)"), 0.0), w=['WAIf%d' % i_ for i_ in range(8)])
            for wi, wsrc in enumerate([lru_w_a, lru_w_i]):
                for d in range(2):
                    for h2 in range(2):
                        P.op('sync', lambda e, wi=wi, d=d, h2=h2, wsrc=wsrc, c=c: e.dma_start(
                            out=WAIf[h2 * 64:(h2 + 1) * 64, wi, d, h2 * 64:(h2 + 1) * 64], in_=wsrc[d, 2 * c + h2]),
                            w=['WAIf%d' % (wi * 4 + d * 2 + h2)], dsem='ldWAI')
            P.op('gpsimd', lambda e, c=c: e.tensor_copy(out=WAI[:, c].rearrange("p w d m -> p (w d m)"), in_=WAIf.rearrange("p w d m -> p (w d m)")),
                 r=['WAIf%d' % i_ for i_ in range(8)], w=['WAI%d' % c])
            emit_wupconv(6)
            XR = IG
            for (n0, nn) in NTL:
                for h in range(2):
                    b = nextbank()
                    for ci in range(8):
                        P.op('tensor', lambda e, ci=ci, h=h, n0=n0, nn=nn, b=b, wb=wb: e.matmul(
                            ps[:, b, 0:nn], lhsT=wb[:, ci, h, :], rhs=Hb[:, ci, n0:n0 + nn], start=(ci == 0), stop=(ci == 7)),
                            r=['WIN%d' % (c % 2)], w=['ps%d' % b])
                    if h == 0:
                        P.op('scalar', lambda e, n0=n0, nn=nn, b=b: e.activation(out=GG[:, n0:n0 + nn], in_=ps[:, b, 0:nn], func=AF.Gelu_apprx_tanh),
                             r=['ps%d' % b], w=rs('GG', n0, n0 + nn))
                    else:
                        P.op('vector', lambda e, n0=n0, nn=nn, b=b: e.tensor_copy(out=IG[:, n0:n0 + nn], in_=ps[:, b, 0:nn]),
                             r=['ps%d' % b], w=rs('IG', n0, n0 + nn))
            for si_, (a_, b_) in enumerate(ESEG):
                seqs_in = [(s0, sl) for (s0, sl) in SEQS if s0 < b_ and s0 + sl > a_]
                rr_ = rs('IG', max(0, a_ - 2), min(NT, b_ + 1)) + ['lconv']
                P.op('vector', lambda e, c=c, a_=a_, b_=b_: e.tensor_scalar(out=XC[:, a_:b_], in0=XR[:, a_:b_], scalar1=lconv[:, 2, c:c + 1],
                                                                       scalar2=lconv[:, 4, c:c + 1], op0=ALU.mult, op1=ALU.add),
                     r=rr_, w=['XC_%d' % si_])
                for (s0, sl) in seqs_in:
                    for k, o in ((0, -2), (1, -1), (3, 1)):
                        lo = max(a_, s0 + max(0, -o)); hi = min(b_, s0 + sl - max(0, o))
                        P.op('vector', lambda e, c=c, k=k, o=o, lo=lo, hi=hi: e.scalar_tensor_tensor(
                            out=XC[:, lo:hi], in0=XR[:, lo + o:hi + o], scalar=lconv[:, k, c:c + 1], in1=XC[:, lo:hi], op0=ALU.mult, op1=ALU.add),
                            r=rr_ + ['XC_%d' % si_], w=['XC_%d' % si_])
            for si_, (a_, b_) in enumerate(ESEG):
                P.op('gpsimd', lambda e, a_=a_, b_=b_: e.tensor_copy(out=XCb[:, a_:b_], in_=XC[:, a_:b_]), r=['XC_%d' % si_], w=['XCb_%d' % si_])
            emit_wupconv(5)
            for d in range(2):
                T = T01[d]
                for (n0, nn) in NTL:
                    for wi in range(2):
                        dstbuf = RA if wi == 0 else IG
                        b = nextbank()
                        P.op('tensor', lambda e, wi=wi, d=d, n0=n0, nn=nn, b=b, c=c: e.matmul(
                            ps[:, b, 0:nn], lhsT=WAI[:, c, wi, d, :], rhs=XCb[:, n0:n0 + nn], start=True, stop=True),
                            r=['WAI%d' % c] + rs('XCb', n0, n0 + nn), w=['ps%d' % b])
                        P.op('scalar', lambda e, wi=wi, d=d, n0=n0, nn=nn, b=b, c=c, dstbuf=dstbuf: e.activation(
                            out=dstbuf[:, n0:n0 + nn], in_=ps[:, b, 0:nn], func=AF.Sigmoid, bias=lba[:, wi, d, c:c + 1], scale=1.0),
                            r=['ps%d' % b, 'lba'], w=rs('RA' if wi == 0 else 'IG', n0, n0 + nn))
                for si_, (a_, b_) in enumerate(ESEG):
                    P.op('scalar', lambda e, d=d, c=c, a_=a_, b_=b_: e.activation(out=RA[:, a_:b_], in_=RA[:, a_:b_], func=AF.Exp, scale=lcl[:, 0, d, c:c + 1]),
                         r=['RA_%d' % si_, 'lcl'], w=['RA_%d' % si_])
                for si_, (a_, b_) in enumerate(ESEG):
                    P.op('gpsimd', lambda e, T=T, a_=a_, b_=b_: e.tensor_tensor(out=T[:, a_:b_], in0=RA[:, a_:b_], in1=RA[:, a_:b_], op=ALU.mult),
                         r=['RA_%d' % si_], w=['T%d_%d' % (d, si_)])
                for si_, (a_, b_) in enumerate(ESEG):
                    P.op('scalar', lambda e, T=T, a_=a_, b_=b_: e.activation(out=T[:, a_:b_], in_=T[:, a_:b_], func=AF.Sqrt, scale=-1.0, bias=1.0),
                         r=['T%d_%d' % (d, si_)], w=['T%d_%d' % (d, si_)])
                for si_, (a_, b_) in enumerate(ESEG):
                    P.op('vector', lambda e, T=T, a_=a_, b_=b_: e.tensor_tensor(out=T[:, a_:b_], in0=T[:, a_:b_], in1=IG[:, a_:b_], op=ALU.mult),
                         r=['T%d_%d' % (d, si_), 'IG_%d' % si_], w=['T%d_%d' % (d, si_)])
                for si_, (a_, b_) in enumerate(ESEG):
                    P.op('gpsimd', lambda e, T=T, a_=a_, b_=b_: e.tensor_tensor(out=T[:, a_:b_], in0=T[:, a_:b_], in1=XC[:, a_:b_], op=ALU.mult),
                         r=['T%d_%d' % (d, si_), 'XC_%d' % si_], w=['T%d_%d' % (d, si_)])
                h0ap = lcl[:, 1, d, c:c + 1]
                if d == 0:
                    plan = [(0, 0, 2047, h0ap, None), (1, 2047, 4096, T[:, 2046:2047], 0), (2, 4096, 4352, 0.0, None), (2, 4352, 4608, 0.0, None)]
                else:
                    plan = [(1, 2047, 4096, h0ap, None), (0, 0, 2047, T[:, 2047:2048], 1), (2, 4096, 4352, 0.0, None), (2, 4352, 4608, 0.0, None)]
                for (si_, a_, b_, init, dep) in plan:
                    rr_ = ['RA_%d' % si_, 'T%d_%d' % (d, si_), 'lcl'] + (['T%d_%d' % (d, dep)] if dep is not None else [])
                    if d == 0:
                        P.op('vector', lambda e, T=T, a_=a_, b_=b_, init=init: e.tensor_tensor_scan(
                            out=T[:, a_:b_], data0=RA[:, a_:b_], data1=T[:, a_:b_], initial=init, op0=ALU.mult, op1=ALU.add),
                            r=rr_, w=['T%d_%d' % (d, si_)])
                    else:
                        P.op('vector', lambda e, T=T, a_=a_, b_=b_, init=init: e.tensor_tensor_scan(
                            out=T[:, a_:b_][:, ::-1], data0=RA[:, a_:b_][:, ::-1], data1=T[:, a_:b_][:, ::-1],
                            initial=init, op0=ALU.mult, op1=ALU.add),
                            r=rr_, w=['T%d_%d' % (d, si_)])
                for pi, (s0, sl) in enumerate(SEQS[1:]):
                    col = s0 + sl - 1 if d == 0 else s0
                    P.op('gpsimd', lambda e, T=T, col=col, pi=pi, d=d, c=c: e.tensor_copy(out=lruout[:, pi, d, c:c + 1], in_=T[:, col:col + 1]),
                         r=['T%d_2' % d], w=['lruout'])
            for si_, (a_, b_) in enumerate(ESEG):
                P.op('gpsimd', lambda e, a_=a_, b_=b_: e.tensor_tensor(out=T01[0][:, a_:b_], in0=T01[0][:, a_:b_], in1=T01[1][:, a_:b_], op=ALU.add),
                     r=['T0_%d' % si_, 'T1_%d' % si_], w=['T0_%d' % si_])
            for si_, (a_, b_) in enumerate(ESEG):
                P.op('vector', lambda e, a_=a_, b_=b_: e.tensor_tensor(out=XCb[:, a_:b_], in0=T01[0][:, a_:b_], in1=GG[:, a_:b_], op=ALU.mult),
                     r=['T0_%d' % si_, 'GG_%d' % si_, 'XCb_%d' % si_], w=['XCb_%d' % si_])
            P.op('sync', lambda e, c=c: e.dma_start(out=Z0[c], in_=XCb), r=['XCb_0', 'XCb_1', 'XCb_2'], w=['Z0_%d' % c], dsem='stZ0')
        for s in range(2):
            for d in range(2):
                P.op('sync', lambda e, s=s, d=d: e.dma_start(out=nlru[s, d].rearrange("(c p) -> p c", p=128), in_=lruout[:, s, d, :]),
                     r=['lruout'], w=['nlru'], dsem='stout')
        emit_wupconv(1000)
        if stop_after == 'C1':
            dump('H_B', Hb, [128, 8, NT], BF16)
        Z0res = ['Z0_%d' % c for c in range(8)]
        if stop_after == 'B':
            P.barrier()
            dump('Z0', Z0, [8, 128, NT], BF16)
            dump('GG', GG, [128, NT], BF16)
            dump('XC', XC, [128, NT])
            dump('RA', RA, [128, NT])
            dump('IG', IG, [128, NT])
            dump('T1', T01[1], [128, NT])
            return finish()
        z0tok = [P.lastw[r_] for r_ in Z0res]
        wuptok = {k: v for k, v in P.lastw.items() if k.startswith('WUP')}
        P.barrier()
        for r_, tk in zip(Z0res, z0tok):
            P.lastw[r_] = tk
        P.lastw.update(wuptok)

        cb = Bump(PBASE, NW)
        XT = cb.f32(8 * D).rearrange("p (j c) -> p j c", j=8)
        AXb = cb.bf(11 * 1024)
        ACTT = AXb.rearrange("p (f n) -> p f n", f=11)
        XN = AXb[:, 0:8 * D].rearrange("p (j c) -> p j c", j=8)

        def actres(fi):
            return ('cXN%d' if fi < 8 else 'ACT%d') % fi
        H2 = cb.bf(8 * 1024).rearrange("p (c n) -> p c n", c=8)
        CV = [cb.f32(1024) for _ in range(2)]
        CG = [cb.f32(1024) for _ in range(2)]
        WU = [cb.bf(8 * 2 * 128).rearrange("p (c h m) -> p c h m", c=8, h=2) for _ in range(3)]
        WD = cb.bf(NF * D).rearrange("p (f n) -> p f n", f=NF)
        GT1 = cb.f32(D); GT2 = cb.f32(D)
        TMP = cb.f32(D)
        WMIX = cb.bf(8 * 2048).rearrange("p (c n) -> p c n", c=8)
        BCA = cb.f32(D); BCB = cb.f32(D)
        UGS = [cb.bf(1024).rearrange("p (g k) -> p g k", g=8) for _ in range(2)]
        phCE_end = cb.cur
        wu_i = [0]

        def ffn(l, t, np_, g):
            ntok = 8 * np_
            rms_rstd(XT, np_, rstd, XN, 'c')
            for j in range(8):
                if j % 2 == 0:
                    P.op('vector', lambda e, j=j: e.tensor_scalar(out=XN[0:np_, j, :], in0=XT[0:np_, j, :], scalar1=rstd[0:np_, j:j + 1],
                                                                  scalar2=None, op0=ALU.mult),
                         r=['cXT%d' % j, 'rstd', 'cXN%d' % j], w=['cXN%d' % j])
                else:
                    P.op('scalar', lambda e, j=j: e.activation(out=XN[0:np_, j, :], in_=XT[0:np_, j, :], func=AF.Copy, scale=rstd[0:np_, j:j + 1]),
                         r=['cXT%d' % j, 'rstd', 'cXN%d' % j], w=['cXN%d' % j])
            to_fm(XN, np_, lambda ci: H2[:, ci, 0:ntok], 'cXN', 'H2_', scale_col=modcol[:, l, g, 2, :], bias_col=modcol[:, l, g, 3, :])
            H2res = ['H2_%d' % ci for ci in range(8)]
            rowlen = 64 if g == 0 else 256
            nrows = ntok // rowlen
            nts = [(0, 512), (512, 512)] if ntok == 1024 else [(0, 512)]
            for fh in range(2):
                for fi in range(11):
                    f = fh * 11 + fi
                    wslot = wu_i[0] % 3; wu_i[0] += 1
                    wu = WU[wslot]
                    P.op('sync', lambda e, f=f, wu=wu: e.dma_start(out=wu, in_=WUP[l, f]), r=['WUP%d_%d_0' % (l, f), 'WUP%d_%d_1' % (l, f)], w=['WU%d' % wslot],
                         dsem='ldWU%d' % wslot)
                    cbuf = f % 2
                    for h in range(2):
                        b0 = nextbank(2)
                        for ni, (n0, nn) in enumerate(nts):
                            for ci in range(8):
                                P.op('tensor', lambda e, ci=ci, h=h, n0=n0, nn=nn, b=b0 + ni, wu=wu: e.matmul(
                                    ps[:, b, 0:nn], lhsT=wu[:, ci, h, :], rhs=H2[:, ci, n0:n0 + nn], start=(ci == 0), stop=(ci == 7)),
                                    r=['WU%d' % wslot] + (H2res if ci == 0 else []), w=['ps%d' % (b0 + ni)])
                        pv = psb(b0, 2)[:, 0:ntok]
                        dst = (CV if h == 0 else CG)[cbuf][:, 0:ntok]
                        fcol = h * NF + f
                        dres = ('CV%d' if h == 0 else 'CG%d') % cbuf
                        pres = ['ps%d' % b0, 'ps%d' % (b0 + 1)]
                        P.op('scalar', lambda e, pv=pv, dst=dst, fcol=fcol: e.activation(
                            out=dst, in_=pv, func=AF.Identity, scale=fconv[:, l, 1, fcol:fcol + 1], bias=fconv[:, l, 3, fcol:fcol + 1]),
                            r=pres + ['fconv'], w=[dres])
                        pv3 = pv.rearrange("p (r n) -> p r n", n=rowlen)
                        d3 = dst.rearrange("p (r n) -> p r n", n=rowlen)
                        P.op('vector', lambda e, pv3=pv3, d3=d3, fcol=fcol: e.scalar_tensor_tensor(
                            out=d3[:, :, 1:rowlen], in0=pv3[:, :, 0:rowlen - 1], scalar=fconv[:, l, 0, fcol:fcol + 1], in1=d3[:, :, 1:rowlen],
                            op0=ALU.mult, op1=ALU.add), r=pres + ['fconv', dres], w=[dres])
                        P.op('vector', lambda e, pv3=pv3, d3=d3, fcol=fcol: e.scalar_tensor_tensor(
                            out=d3[:, :, 0:rowlen - 1], in0=pv3[:, :, 1:rowlen], scalar=fconv[:, l, 2, fcol:fcol + 1], in1=d3[:, :, 0:rowlen - 1],
                            op0=ALU.mult, op1=ALU.add), r=pres + ['fconv', dres], w=[dres])
                    cv = CV[cbuf][:, 0:ntok]; cg = CG[cbuf][:, 0:ntok]
                    P.op('scalar', lambda e, cg=cg: e.activation(out=cg, in_=cg, func=AF.Silu), r=['CG%d' % cbuf], w=['CG%d' % cbuf])
                    P.op('gpsimd', lambda e, cv=cv, cg=cg, fi=fi: e.tensor_tensor(out=ACTT[:, fi, 0:ntok], in0=cv, in1=cg, op=ALU.mult),
                         r=['CV%d' % cbuf, 'CG%d' % cbuf], w=[actres(fi)])
                for j in range(8):
                    b0 = nextbank(2)
                    for n in range(2):
                        for fi in range(11):
                            f = fh * 11 + fi
                            P.op('tensor', lambda e, fi=fi, f=f, j=j, n=n, b=b0 + n: e.matmul(
                                ps[0:np_, b, :], lhsT=ACTT[:, fi, j:ntok:8], rhs=WD[:, f, n * 512:(n + 1) * 512], start=(fi == 0), stop=(fi == 10)),
                                r=[actres(fi), 'WD'], w=['ps%d' % (b0 + n)])
                    pv = psb(b0, 2)
                    pres = ['ps%d' % b0, 'ps%d' % (b0 + 1)]
                    P.op('vector', lambda e, pv=pv, j=j: e.tensor_tensor(out=XT[0:np_, j, :], in0=XT[0:np_, j, :], in1=pv[0:np_, :], op=ALU.add),
                         r=pres + ['cXT%d' % j], w=['cXT%d' % j])

        def load_wd(l, np_):
            for half in range(2):
                P.op('gpsimd', lambda e, half=half: e.dma_start(out=WD[:, half * 11:(half + 1) * 11, :],
                                                               in_=ffn_w_down[l, half * 11 * 128:(half + 1) * 11 * 128, :].rearrange("(f p) n -> p f n", p=128)),
                     w=['WD'], dsem='ldWD')
            for half in range(2):
                P.op('vector', lambda e, half=half: e.tensor_tensor(out=WD[:, half * 11:(half + 1) * 11, :], in0=WD[:, half * 11:(half + 1) * 11, :],
                                                                  in1=GT2.unsqueeze(1).broadcast_to([128, 11, D]), op=ALU.mult),
                     r=['WD', 'GT2'], w=['WD'])

        def load_wmix(l):
            if l == 0:
                for half in range(2):
                    P.op('gpsimd', lambda e, half=half: e.dma_start(out=WMIX[:, half * 4:(half + 1) * 4, 0:D],
                                                                   in_=lru_w_out[half * 512:(half + 1) * 512, :].rearrange("(c p) n -> p c n", p=128)),
                         w=['WMIX'], dsem='ldWMIX')
            else:
                for q in range(4):
                    P.op('gpsimd', lambda e, q=q: e.dma_start(out=WMIX[:, q * 2:(q + 1) * 2, :],
                                                             in_=s5_w_glu[q * 256:(q + 1) * 256, :].rearrange("(c p) n -> p c n", p=128)),
                         w=['WMIX'], dsem='ldWMIX')
            P.op('vector', lambda e: e.tensor_tensor(out=WMIX[:, :, 0:D], in0=WMIX[:, :, 0:D], in1=GT1.unsqueeze(1).broadcast_to([128, 8, D]), op=ALU.mult),
                 r=['WMIX', 'GT1'], w=['WMIX'])

        XTres = ['cXT%d' % j for j in range(8)]
        JUNK1 = UGS[1].rearrange("p g k -> p (g k)").unsqueeze(1).broadcast_to([128, 8, 1024])
        if stop_after == 'C0':
            dump('Z0a', Z0, [8, 128, NT], BF16, r=Z0res)
            dump('T0', T01[0], [128, NT])
            dump('T1', T01[1], [128, NT])
            dump('GG', GG, [128, NT], BF16)
            dump('RA', RA, [128, NT])
            dump('XCb', XCb, [128, NT], BF16)
            dump('H', Hb, [128, 8, NT], BF16)
            dump('WIN1', WIN[1], [128, 8, 2, 128], BF16)
            P.op('sync', lambda e: e.dma_start(out=H2[:, :, 0:1024], in_=Z0[:, :, 0:1024].rearrange("c p n -> p c n")),
                 r=Z0res, w=['zt'], dsem='ldZT')
            dump('zt', H2, [128, 8, 1024], BF16, r=['zt'])
            dump('Z0b', Z0, [8, 128, NT], BF16, r=['zt'])
            for half in range(2):
                P.op('gpsimd', lambda e, half=half: e.dma_start(out=WD[:, half * 11:(half + 1) * 11, :],
                                                               in_=ffn_w_down[0, half * 11 * 128:(half + 1) * 11 * 128, :].rearrange("(f p) n -> p f n", p=128)),
                     w=['WD'], dsem='ldWD')
            dump('Z0c', Z0, [8, 128, NT], BF16, r=['WD'])
            return finish()
        ZT = H2

        def load_zt(t_):
            base_, np2 = TT[t_]
            nt_ = 8 * np2
            P.op('sync', lambda e: e.dma_start(out=ZT[:, :, 0:nt_], in_=Z0[:, :, base_:base_ + nt_].rearrange("c p n -> p c n")),
                 r=Z0res, w=['H2_%d' % ci for ci in range(8)], dsem='ldZT')
        for t in range(5):
            base, np_ = TT[t]
            g = 0 if t < 4 else 1
            ntok = 8 * np_
            if t == 0 or t == 4:
                load_bc(GT1, 0, g, 0, 128, 'GT1')
                load_bc(GT2, 0, g, 1, 128, 'GT2')
                load_wd(0, np_)
                load_wmix(0)
            P.op('sync', lambda e, t=t, np_=np_: e.dma_start(out=XT[0:np_], in_=xtile_dram(t)), w=XTres, dsem='ldXT')
            if t == 0:
                load_zt(0)
            for j in range(8):
                b0 = nextbank(2)
                for n in range(2):
                    for c in range(8):
                        P.op('tensor', lambda e, c=c, j=j, n=n, b=b0 + n, ntok=ntok, np_=np_: e.matmul(
                            ps[0:np_, b, :], lhsT=ZT[:, c, j:ntok:8], rhs=WMIX[:, c, n * 512:(n + 1) * 512], start=(c == 0), stop=(c == 7)),
                            r=['WMIX'] + (['H2_%d' % ci for ci in range(8)] if c == 0 else []), w=['ps%d' % (b0 + n)])
                pv = psb(b0, 2)
                pres = ['ps%d' % b0, 'ps%d' % (b0 + 1)]
                P.op('vector', lambda e, pv=pv, np_=np_, j=j: e.tensor_tensor(out=XT[0:np_, j, :], in0=XT[0:np_, j, :], in1=pv[0:np_, :], op=ALU.add),
                     r=pres + ['cXT%d' % j], w=['cXT%d' % j])
            if stop_after == 'C1' and t == 0:
                dump('x1', XT, [128, 8, D], r=XTres)
                dump('gt1', GT1, [128, D], r=['GT1'])
                dump('wmix', WMIX, [128, 8, 2048], BF16, r=['WMIX'])
                dump('zt', ZT, [128, 8, 1024], BF16, r=['H2_%d' % ci for ci in range(8)])
            ffn(0, t, np_, g)
            if stop_after == 'C1' and t == 0:
                dump('x2', XT, [128, 8, D], r=XTres)
                dump('h2', H2, [128, 8, 1024], BF16, r=['H2_%d' % ci for ci in range(8)])
                dump('actt', ACTT, [128, 11, 1024], BF16, r=[actres(fi) for fi in range(11)])
                dump('cv', CV[1], [128, 1024], r=['CV1'])
                dump('cg', CG[1], [128, 1024], r=['CG1'])
                dump('wd', WD, [128, NF, D], BF16, r=['WD'])
                dump('wu', WU[0], [128, 8, 2, 128], BF16, r=['WU0'])
                return finish()
            if t < 4:
                load_zt(t + 1)
            if stop_after not in ('C', 'C1'):
                if t == 0 or t == 4:
                    load_bc(BCA, 1, g, 2, np_, 'BCA')
                    load_bc(BCB, 1, g, 3, np_, 'BCB')
                rms_rstd(XT, np_, rstd, JUNK1, 'c', jres=lambda j: 'UGS1')
                Ugm = AXb[:, 0:8 * D].rearrange("p (g j c) -> p g j c", g=64, j=8)
                ugres = ['cXN%d' % j for j in range(8)]
                for j in range(8):
                    P.op('vector', lambda e, j=j, np_=np_: e.scalar_tensor_tensor(out=TMP[0:np_, :], in0=XT[0:np_, j, :], scalar=rstd[0:np_, j:j + 1],
                                                                             in1=BCA[0:np_, :], op0=ALU.mult, op1=ALU.mult),
                         r=['cXT%d' % j, 'rstd', 'BCA'], w=['TMP'])
                    P.op('vector', lambda e, j=j, np_=np_: e.tensor_tensor(out=Ugm[0:np_, :, j, :], in0=TMP[0:np_, :].rearrange("p (g c) -> p g c", g=64),
                                                                         in1=BCB[0:np_, :].rearrange("p (g c) -> p g c", g=64), op=ALU.add),
                         r=['TMP', 'BCB'], w=ugres)
                kb = t * 128 if t < 4 else 512
                for g8 in range(8):
                    b = nextbank()
                    for gi in range(8):
                        gg_ = g8 * 8 + gi
                        P.op('tensor', lambda e, gg_=gg_, gi=gi, b=b, np_=np_: e.transpose(
                            out=psbf(b)[:, gi * np_:(gi + 1) * np_], in_=Ugm[0:np_, gg_, :, :].rearrange("p j c -> p (j c)"),
                            identity=ident[0:np_, 0:np_]), r=ugres + ['ident'], w=['ps%d' % b])
                    us = UGS[g8 % 2]
                    if g8 % 2 == 0:
                        P.op('scalar', lambda e, b=b, np_=np_, us=us: e.activation(
                            out=us[:, :, 0:np_], in_=psbf(b)[:, 0:8 * np_].rearrange("p (g k) -> p g k", g=8), func=AF.Copy),
                            r=['ps%d' % b], w=['UGS%d' % (g8 % 2)])
                    else:
                        P.op('vector', lambda e, b=b, np_=np_, us=us: e.tensor_copy(
                            out=us[:, :, 0:np_], in_=psbf(b)[:, 0:8 * np_].rearrange("p (g k) -> p g k", g=8)),
                            r=['ps%d' % b], w=['UGS%d' % (g8 % 2)])
                    P.op('scalar', lambda e, g8=g8, kb=kb, np_=np_, us=us: e.dma_start(
                        out=UG[g8 * 8:(g8 + 1) * 8, :, kb:kb + np_].rearrange("g p k -> p g k"), in_=us[:, :, 0:np_]),
                        r=['UGS%d' % (g8 % 2)], w=['UGd'], dsem='stUG%d' % (g8 % 2))
                P.op('sync', lambda e, t=t, np_=np_: e.dma_start(out=xtile_dram(t, scr=True), in_=XT[0:np_]), r=XTres, w=['Xd%d' % t], dsem='stX')
            if stop_after == 'C':
                dst = (dbg['xs1'][base:base + 1024, :] if t < 4 else dbg['xp1'][:, :]).rearrange("(k j) c -> k j c", j=8)
                P.op('sync', lambda e, dst=dst, np_=np_: e.dma_start(out=dst, in_=XT[0:np_]), r=XTres, w=['dbgout'], dsem='stX')
        if stop_after in ('C', 'C1'):
            return finish()
        keep = {k: v for k, v in P.lastw.items() if k.startswith('WUP') or k.startswith('Xd') or k == 'UGd'}
        P.barrier()
        P.lastw.update(keep)

        TWO_PI = 6.283185307179586
        MAGIC = 12582912.0
        db = Bump(PBASE, NW)
        QT = db.bf(2 * 64 * 2 * 64).rearrange("p (d g r m) -> p d g r m", d=2, g=64, r=2)
        PT = db.bf(2 * 32 * 2 * 128).rearrange("p (d g r m) -> p d g r m", d=2, g=32, r=2)
        MT = db.bf(64 * 128).rearrange("p (g m) -> p g m", g=64)
        ARt = db.f32(128); AIt = db.f32(128)
        Hs = db.f32(128); Zs = db.f32(128); T1 = db.f32(128); T2 = db.f32(128)
        MASKF = db.f32(128); MASKB = db.f32(128); DCOL = db.f32(64)
        SEL = db.f32(128); TRI = db.f32(128)
        H0 = db.f32(128)
        H0b = db.bf(128)
        d_base = db.cur
        Qb = db.bf(2 * 32 * 2 * 128).rearrange("p (d g r m) -> p d g r m", d=2, g=32, r=2)
        BR, BI, CR, CI = [db.f32(2 * 32 * 16).rearrange("p (d g c) -> p d g c", d=2, g=32) for _ in range(4)]
        CN = db.f32(4 * 8 * 64).rearrange("p (a h w q) -> p a h w q", a=4, h=4, w=2)
        TA = db.f32(2048); TB = db.f32(2048)

        def t3():
            return db.f32(64).rearrange("p (d g) -> p d g", d=2)

        def t4():
            return db.f32(512).rearrange("p (d g e) -> p d g e", d=2, g=32)
        LDT = t3(); LR = t3(); TH = t3(); LAR = t3(); LAI = t3(); NR_ = t3(); L2 = t3(); BETR = t3(); BETI = t3(); TM3 = t3(); TM3b = t3()
        ANG = t4(); KK = t4(); SN = t4(); SH = t4(); CS = t4(); LRE = t4(); MG = t4(); MI = t4()
        WPr = t4(); WPi = t4(); WQr = t4(); WQi = t4(); TM4 = t4()

        def f3(a):
            return a.rearrange("p d g -> p (d g)")

        def f4(a):
            return a.rearrange("p d g e -> p (d g e)")

        def vop(fn, r, w, eng='vector'):
            P.op(eng, fn, r=r, w=w)

        G_BR = ['BR%d%d' % (d_, g_) for d_ in range(2) for g_ in range(2)]
        G_BI = ['BI%d%d' % (d_, g_) for d_ in range(2) for g_ in range(2)]
        G_LAR = ['LAR%d%d' % (d_, g_) for d_ in range(2) for g_ in range(2)]
        G_LAI = ['LAI%d%d' % (d_, g_) for d_ in range(2) for g_ in range(2)]
        G_LDT = ['LDT%d%d' % (d_, g_) for d_ in range(2) for g_ in range(2)]
        G_H0 = ['H0%s%d%d' % (x_, d_, g_) for x_ in 'ri' for d_ in range(2) for g_ in range(2)]
        G_CN = ['CN%d%d%d' % (d_, r__, g_) for d_ in range(2) for r__ in range(2) for g_ in range(2)]
        G_DCOL = ['DCOL%d' % j_ for j_ in range(8)]
        for d in range(2):
            for gh in range(2):
                ps_ = slice(gh * 64, (gh + 1) * 64)
                gs_ = slice(gh * 32, (gh + 1) * 32)
                P.op('sync', lambda e, d=d, ps_=ps_, gs_=gs_: e.dma_start(out=BR[ps_, d], in_=s5_b_re[d, gs_].rearrange("g p c -> p g c")), w=['BR%d%d' % (d, gh)], dsem='ldS5_BR')
                P.op('sync', lambda e, d=d, ps_=ps_, gs_=gs_: e.dma_start(out=BI[ps_, d], in_=s5_b_im[d, gs_].rearrange("g p c -> p g c")), w=['BI%d%d' % (d, gh)], dsem='ldS5_BI')
                P.op('sync', lambda e, d=d, ps_=ps_, gs_=gs_: e.dma_start(out=LAR[ps_, d], in_=s5_a_re[d, gs_].rearrange("g p -> p g")), w=['LAR%d%d' % (d, gh)], dsem='ldS5_LAR')
                P.op('sync', lambda e, d=d, ps_=ps_, gs_=gs_: e.dma_start(out=LAI[ps_, d], in_=s5_a_im[d, gs_].rearrange("g p -> p g")), w=['LAI%d%d' % (d, gh)], dsem='ldS5_LAI')
                P.op('sync', lambda e, d=d, ps_=ps_, gs_=gs_: e.dma_start(out=LDT[ps_, d], in_=s5_log_dt[d:d + 1, gs_].broadcast_to([64, 32])), w=['LDT%d%d' % (d, gh)], dsem='ldS5_LDT')
                P.op('sync', lambda e, d=d, ps_=ps_, gs_=gs_: e.dma_start(out=H0[ps_].rearrange("p (d r g) -> p d r g", d=2, r=2)[:, d, 0, :],
                                                                        in_=st_re[d, gs_].rearrange("g p -> p g")), w=['H0r%d%d' % (d, gh)], dsem='ldS5_H0')
                P.op('sync', lambda e, d=d, ps_=ps_, gs_=gs_: e.dma_start(out=H0[ps_].rearrange("p (d r g) -> p d r g", d=2, r=2)[:, d, 1, :],
                                                                        in_=st_im[d, gs_].rearrange("g p -> p g")), w=['H0i%d%d' % (d, gh)], dsem='ldS5_H0')
            for ri, csrc in enumerate([s5_c_re, s5_c_im]):
                for gh in range(2):
                    P.op('sync', lambda e, d=d, ri=ri, csrc=csrc, gh=gh: e.dma_start(
                        out=CN[:, d * 2 + ri, :, gh, :], in_=csrc[d, gh * 32:(gh + 1) * 32].rearrange("(h l) c p -> (l c) h p", l=8)),
                        w=['CN%d%d%d' % (d, ri, gh)], dsem='ldS5_CN')
        for j in range(8):
            P.op('sync', lambda e, j=j: e.dma_start(out=DCOL[j * 16:(j + 1) * 16, :], in_=s5_d[0].rearrange("(g c) -> c g", c=16)), w=['DCOL%d' % j], dsem='ldS5_DC')
        for d in range(2):
            for ri in range(2):
                b = nextbank()
                for h4 in range(4):
                    src = CN[:, d * 2 + ri, h4, :, :].rearrange("p w q -> p (w q)")
                    P.op('tensor', lambda e, src=src, b=b, h4=h4: e.transpose(out=ps[:, b, h4 * 128:(h4 + 1) * 128], in_=src,
                                                                            identity=identf), r=G_CN + ['identf'], w=['ps%d' % b])
                dstC = (CR if ri == 0 else CI)[:, d].rearrange("p g c -> p (g c)")
                P.op('vector', lambda e, dstC=dstC, b=b: e.tensor_copy(out=dstC, in_=ps[:, b, :]), r=['ps%d' % b], w=['CR' if ri == 0 else 'CI'])
        P.op('gpsimd', lambda e: e.memset(SEL[0:8, :], 1.0), w=['SEL'])
        P.op('gpsimd', lambda e: e.memset(TRI[0:8, :], 1.0), w=['TRI'])
        P.op('gpsimd', lambda e: e.affine_select(out=SEL[0:8, :], in_=SEL[0:8, :], pattern=[[1, 8], [0, 16]], compare_op=ALU.is_equal,
                                                 fill=0.0, base=0, channel_multiplier=-1), r=['SEL'], w=['SEL'])
        P.op('gpsimd', lambda e: e.affine_select(out=TRI[0:8, :], in_=TRI[0:8, :], pattern=[[1, 8], [0, 16]], compare_op=ALU.is_ge,
                                                 fill=0.0, base=0, channel_multiplier=-1), r=['TRI'], w=['TRI'])
        bm = nextbank()
        P.op('tensor', lambda e: e.matmul(ps[:, bm, 0:128], lhsT=SEL[0:8, :], rhs=TRI[0:8, :], start=True, stop=True), r=['SEL', 'TRI'], w=['ps%d' % bm])
        P.op('tensor', lambda e: e.matmul(ps[:, bm, 128:256], lhsT=TRI[0:8, :], rhs=SEL[0:8, :], start=True, stop=True), r=['SEL', 'TRI'], w=['ps%d' % bm])
        P.op('vector', lambda e: e.tensor_copy(out=MASKF, in_=ps[:, bm, 0:128]), r=['ps%d' % bm], w=['MASK'])
        P.op('vector', lambda e: e.tensor_copy(out=MASKB, in_=ps[:, bm, 128:256]), r=['ps%d' % bm], w=['MASK'])

        S = 'scalar'; V = 'vector'
        P.op(S, lambda e: e.activation(out=f3(LDT), in_=f3(LDT), func=AF.Exp), r=G_LDT, w=['LDTx'])
        vop(lambda e: e.tensor_tensor(out=f3(LR), in0=f3(LAR), in1=f3(LDT), op=ALU.mult), G_LAR + ['LDTx'], ['LR'])
        vop(lambda e: e.tensor_tensor(out=f3(TH), in0=f3(LAI), in1=f3(LDT), op=ALU.mult), G_LAI + ['LDTx'], ['TH'])
        for ei in range(8):
            vop(lambda e, ei=ei: e.tensor_scalar(out=ANG[:, :, :, ei], in0=TH, scalar1=float(ei + 1), scalar2=None, op0=ALU.mult), ['TH'], ['ANG'])
            vop(lambda e, ei=ei: e.tensor_scalar(out=LRE[:, :, :, ei], in0=LR, scalar1=float(ei + 1), scalar2=None, op0=ALU.mult), ['LR'], ['LRE'])
        vop(lambda e: e.tensor_scalar(out=f4(KK), in0=f4(ANG), scalar1=1.0 / TWO_PI, scalar2=MAGIC, op0=ALU.mult, op1=ALU.add), ['ANG'], ['KK'])
        vop(lambda e: e.tensor_scalar(out=f4(KK), in0=f4(KK), scalar1=-MAGIC, scalar2=None, op0=ALU.add), ['KK'], ['KK'])
        vop(lambda e: e.scalar_tensor_tensor(out=f4(ANG), in0=f4(KK), scalar=-TWO_PI, in1=f4(ANG), op0=ALU.mult, op1=ALU.add), ['KK', 'ANG'], ['ANG'])
        P.op(S, lambda e: e.activation(out=f4(SN), in_=f4(ANG), func=AF.Sin, scale=0.999999), r=['ANG'], w=['SN'])
        P.op(S, lambda e: e.activation(out=f4(SH), in_=f4(ANG), func=AF.Sin, scale=0.5), r=['ANG'], w=['SH'])
        vop(lambda e: e.tensor_tensor(out=f4(SH), in0=f4(SH), in1=f4(SH), op=ALU.mult), ['SH'], ['SH'])
        vop(lambda e: e.tensor_scalar(out=f4(CS), in0=f4(SH), scalar1=-2.0, scalar2=1.0, op0=ALU.mult, op1=ALU.add), ['SH'], ['CS'])
        P.op(S, lambda e: e.activation(out=f4(MG), in_=f4(LRE), func=AF.Exp), r=['LRE'], w=['MG'])
        P.op(S, lambda e: e.activation(out=f4(MI), in_=f4(LRE), func=AF.Exp, scale=-1.0), r=['LRE'], w=['MI'])
        vop(lambda e: e.tensor_tensor(out=f4(WPr), in0=f4(MG), in1=f4(CS), op=ALU.mult), ['MG', 'CS'], ['WPr'])
        vop(lambda e: e.tensor_tensor(out=f4(WPi), in0=f4(MG), in1=f4(SN), op=ALU.mult), ['MG', 'SN'], ['WPi'])
        vop(lambda e: e.tensor_scalar(out=NR_, in0=WPr[:, :, :, 0], scalar1=-1.0, scalar2=None, op0=ALU.add), ['WPr'], ['NR'])
        NI_ = WPi[:, :, :, 0]
        vop(lambda e: e.tensor_tensor(out=f3(L2), in0=f3(LAR), in1=f3(LAR), op=ALU.mult), G_LAR, ['L2'])
        vop(lambda e: e.tensor_tensor(out=f3(TM3), in0=f3(LAI), in1=f3(LAI), op=ALU.mult), G_LAI, ['TM3'])
        vop(lambda e: e.tensor_tensor(out=f3(L2), in0=f3(L2), in1=f3(TM3), op=ALU.add), ['L2', 'TM3'], ['L2'])
        vop(lambda e: e.reciprocal(out=f3(L2), in_=f3(L2)), ['L2'], ['L2'])
        vop(lambda e: e.tensor_tensor(out=BETR, in0=NR_, in1=LAR, op=ALU.mult), ['NR'] + G_LAR, ['BETR'])
        vop(lambda e: e.tensor_tensor(out=TM3, in0=NI_, in1=LAI, op=ALU.mult), ['WPi', 'L2'] + G_LAI, ['TM3'])
        vop(lambda e: e.tensor_tensor(out=f3(BETR), in0=f3(BETR), in1=f3(TM3), op=ALU.add), ['BETR', 'TM3'], ['BETR'])
        vop(lambda e: e.tensor_tensor(out=f3(BETR), in0=f3(BETR), in1=f3(L2), op=ALU.mult), ['BETR', 'L2'], ['BETR'])
        vop(lambda e: e.tensor_tensor(out=BETI, in0=NI_, in1=LAR, op=ALU.mult), ['WPi'] + G_LAR, ['BETI'])
        vop(lambda e: e.tensor_tensor(out=f3(TM3b), in0=f3(NR_), in1=f3(LAI), op=ALU.mult), ['NR'] + G_LAI, ['TM3b'])
        vop(lambda e: e.tensor_tensor(out=f3(BETI), in0=f3(BETI), in1=f3(TM3b), op=ALU.subtract), ['BETI', 'TM3b'], ['BETI'])
        vop(lambda e: e.tensor_tensor(out=f3(BETI), in0=f3(BETI), in1=f3(L2), op=ALU.mult), ['BETI', 'L2'], ['BETI'])
        def bc4(a):
            return a.unsqueeze(3).broadcast_to([128, 2, 32, 8])
        vop(lambda e: e.tensor_tensor(out=WQr, in0=CS, in1=bc4(BETR), op=ALU.mult), ['CS', 'BETR'], ['WQr'])
        vop(lambda e: e.tensor_tensor(out=TM4, in0=SN, in1=bc4(BETI), op=ALU.mult), ['SN', 'BETI'], ['TM4'])
        vop(lambda e: e.tensor_tensor(out=f4(WQr), in0=f4(WQr), in1=f4(TM4), op=ALU.add), ['WQr', 'TM4'], ['WQr'])
        vop(lambda e: e.tensor_tensor(out=f4(WQr), in0=f4(WQr), in1=f4(MI), op=ALU.mult), ['WQr', 'MI'], ['WQr'])
        vop(lambda e: e.tensor_tensor(out=WQi, in0=CS, in1=bc4(BETI), op=ALU.mult), ['CS', 'BETI'], ['WQi'])
        vop(lambda e: e.tensor_tensor(out=TM4, in0=SN, in1=bc4(BETR), op=ALU.mult), ['SN', 'BETR', 'WQr'], ['TM4'])
        vop(lambda e: e.tensor_tensor(out=f4(WQi), in0=f4(WQi), in1=f4(TM4), op=ALU.subtract), ['WQi', 'TM4'], ['WQi'])
        vop(lambda e: e.tensor_tensor(out=f4(WQi), in0=f4(WQi), in1=f4(MI), op=ALU.mult), ['WQi', 'MI'], ['WQi'])
        ARv = ARt.rearrange("p (d r g) -> p d r g", d=2, r=2)
        AIv = AIt.rearrange("p (d r g) -> p d r g", d=2, r=2)
        for r_ in range(2):
            vop(lambda e, r_=r_: e.tensor_copy(out=ARv[:, :, r_, :], in_=WPr[:, :, :, 7]), ['WPr'], ['ARt'])
        vop(lambda e: e.tensor_scalar(out=AIv[:, :, 0, :], in0=WPi[:, :, :, 7], scalar1=-1.0, scalar2=None, op0=ALU.mult), ['WPi'], ['AIt'])
        vop(lambda e: e.tensor_copy(out=AIv[:, :, 1, :], in_=WPi[:, :, :, 7]), ['WPi'], ['AIt'])

        def jord(W, d, gq):
            w_ = W[:, d, gq * 16:(gq + 1) * 16]
            if d == 1:
                w_ = w_[:, :, ::-1]
            return w_.unsqueeze(3).broadcast_to([128, 16, 8, 16])

        def bcj(X, d, gq):
            return X[:, d, gq * 16:(gq + 1) * 16].unsqueeze(2).broadcast_to([128, 16, 8, 16])
        TA4 = TA.rearrange("p (g j c) -> p g j c", g=16, j=8); TB4 = TB.rearrange("p (g j c) -> p g j c", g=16, j=8)

        def cplx_table(dst, d, Xr, Xi, Wr, Wi, xres, negim):
            for gq in range(2):
                gsl_ = slice(gq * 16, (gq + 1) * 16)
                o0 = dst[:, d, gsl_, 0, :].rearrange("p g (j c) -> p g j c", j=8)
                o1 = dst[:, d, gsl_, 1, :].rearrange("p g (j c) -> p g j c", j=8)
                P.op('vector', lambda e, gq=gq: e.tensor_tensor(out=TA4, in0=bcj(Xr, d, gq), in1=jord(Wr, d, gq), op=ALU.mult), r=xres + ['tbl'], w=['TA'])
                P.op('gpsimd', lambda e, gq=gq: e.tensor_tensor(out=TB4, in0=bcj(Xi, d, gq), in1=jord(Wi, d, gq), op=ALU.mult), r=xres + ['tbl'], w=['TB'])
                P.op('vector', lambda e, o0=o0: e.tensor_tensor(out=o0, in0=TA4, in1=TB4, op=ALU.subtract), r=['TA', 'TB'], w=['tbl'])
                P.op('vector', lambda e, gq=gq: e.tensor_tensor(out=TA4, in0=bcj(Xr, d, gq), in1=jord(Wi, d, gq), op=ALU.mult), r=xres + ['tbl'], w=['TA'])
                P.op('gpsimd', lambda e, gq=gq: e.tensor_tensor(out=TB4, in0=bcj(Xi, d, gq), in1=jord(Wr, d, gq), op=ALU.mult), r=xres + ['tbl'], w=['TB'])
                if negim:
                    P.op('vector', lambda e, o1=o1: e.scalar_tensor_tensor(out=o1, in0=TA4, scalar=-1.0, in1=TB4, op0=ALU.mult, op1=ALU.subtract),
                         r=['TA', 'TB'], w=['tbl'])
                else:
                    P.op('vector', lambda e, o1=o1: e.tensor_tensor(out=o1, in0=TA4, in1=TB4, op=ALU.add), r=['TA', 'TB'], w=['tbl'])
        for d in range(2):
            cplx_table(Qb, d, BR, BI, WQr, WQi, G_BR + G_BI + ['WQr', 'WQi'], False)
            cplx_table(PT, d, CR, CI, WPr, WPi, ['CR', 'CI', 'WPr', 'WPi'], True)

        TA3 = TA[:, 0:512].rearrange("p (g m) -> p g m", g=4); TB3 = TB[:, 0:512].rearrange("p (g m) -> p g m", g=4)
        for g4 in range(16):
            bf_ = nextbank(); bb_ = nextbank()
            for gi in range(4):
                g = g4 * 4 + gi
                gh, g32 = g // 32, g % 32
                psl = slice(gh * 64, (gh + 1) * 64)
                for d, bk in ((0, bf_), (1, bb_)):
                    for r_ in range(2):
                        P.op('tensor', lambda e, d=d, bk=bk, r_=r_, gi=gi, g32=g32, psl=psl: e.matmul(
                            ps[:, bk, gi * 128:(gi + 1) * 128], lhsT=Qb[psl, d, g32, r_, :], rhs=PT[psl, d, g32, r_, :], start=(r_ == 0), stop=(r_ == 1)),
                            r=['tbl'], w=['ps%d' % bk])
            gsl = slice(g4 * 4, g4 * 4 + 4)
            mf = MASKF.unsqueeze(1).broadcast_to([128, 4, 128]); mb = MASKB.unsqueeze(1).broadcast_to([128, 4, 128])
            idb = identf.unsqueeze(1).broadcast_to([128, 4, 128])
            dcb = DCOL[:, gsl].unsqueeze(2).broadcast_to([128, 4, 128])
            P.op('vector', lambda e, bf_=bf_, mf=mf: e.tensor_tensor(out=TA3, in0=ps[:, bf_, :].rearrange("p (g m) -> p g m", g=4), in1=mf, op=ALU.mult),
                 r=['ps%d' % bf_, 'MASK'], w=['TA'])
            P.op('vector', lambda e, bb_=bb_, mb=mb: e.tensor_tensor(out=TB3, in0=ps[:, bb_, :].rearrange("p (g m) -> p g m", g=4), in1=mb, op=ALU.mult),
                 r=['ps%d' % bb_, 'MASK'], w=['TB'])
            P.op('gpsimd', lambda e: e.tensor_tensor(out=TA3, in0=TA3, in1=TB3, op=ALU.add), r=['TA', 'TB'], w=['TA'])
            P.op('gpsimd', lambda e, idb=idb, dcb=dcb: e.tensor_tensor(out=TB3, in0=idb, in1=dcb, op=ALU.mult), r=['TA', 'identf'] + G_DCOL, w=['TB'])
            P.op('vector', lambda e, gsl=gsl: e.tensor_tensor(out=MT[:, gsl, :], in0=TA3, in1=TB3, op=ALU.add), r=['TA', 'TB'], w=['MT'])
        for d in range(2):
            for g8 in range(8):
                b = nextbank()
                for gi in range(8):
                    g = g8 * 8 + gi
                    gh, g32 = g // 32, g % 32
                    psl = slice(gh * 64, (gh + 1) * 64)
                    for r_ in range(2):
                        P.op('tensor', lambda e, d=d, b=b, gi=gi, r_=r_, g32=g32, psl=psl: e.transpose(
                            out=psbf(b)[:, (gi * 2 + r_) * 64:(gi * 2 + r_ + 1) * 64], in_=Qb[psl, d, g32, r_, :], identity=ident[psl, psl]),
                            r=['tbl', 'ident'], w=['ps%d' % b])
                P.op('scalar', lambda e, d=d, g8=g8, b=b: e.activation(out=QT[:, d, g8 * 8:(g8 + 1) * 8].rearrange("p g r m -> p (g r m)"),
                                                                      in_=psbf(b), func=AF.Copy), r=['ps%d' % b], w=['QT'])
        P.barrier()

        sc = Bump(d_base, NW)
        KC = 64
        SB_ = sc.f32(2 * 2 * 32 * KC).rearrange("p (d r g k) -> p d r g k", d=2, r=2, g=32)
        HBF = sc.f32(2 * 2 * 32 * KC).rearrange("p (d r g k) -> p d r g k", d=2, r=2, g=32)
        HSB = sc.bf(2 * 2 * 32 * KC).rearrange("p (d r g k) -> p d r g k", d=2, r=2, g=32)
        UGF = [sc.bf(64 * KC).rearrange("p (g k) -> p g k", g=64) for _ in range(2)]
        ZER = sc.bf(128)
        P.op('gpsimd', lambda e: e.memset(ZER, 0.0), w=['ZER'])
        Zsw = Zs.rearrange("p (d r g) -> p d r g", d=2, r=2)[:, :, ::-1, :]
        T2v = T2.rearrange("p (d r g) -> p d r g", d=2, r=2)
        AIv4 = AIt.rearrange("p (d r g) -> p d r g", d=2, r=2)
        TT4 = sc.f32(256).rearrange("p (d t s g) -> p d t s g", d=2, t=2, s=2)
        WW4 = sc.f32(256).rearrange("p (d t s g) -> p d t s g", d=2, t=2, s=2)
        ARv4 = ARt.rearrange("p (d r g) -> p d r g", d=2, r=2)
        P.op('vector', lambda e: e.tensor_copy(out=WW4[:, :, 0], in_=ARv4), r=['ARt'], w=['WW'])
        P.op('vector', lambda e: e.tensor_copy(out=WW4[:, :, 1], in_=AIv4[:, :, ::-1, :]), r=['AIt'], w=['WW'])
        SEQC = [(0, 512, 0), (512, 32, 513), (544, 32, 546)]
        for si, (c0, n, hb) in enumerate(SEQC):
            kc = min(KC, n)
            nb = n // kc
            if si == 0:
                P.op('vector', lambda e: e.tensor_copy(out=Hs, in_=H0), r=G_H0, w=['Hs'])
                P.op('gpsimd', lambda e: e.tensor_copy(out=H0b, in_=H0), r=G_H0, w=['H0b'])
                h0src = H0b
                h0res = 'H0b'
            else:
                P.op('vector', lambda e: e.memset(Hs, 0.0), w=['Hs'])
                h0src = ZER
                h0res = 'ZER'
            h0v = h0src.rearrange("p (d q) -> p d q", d=2)
            P.op('sync', lambda e, hb=hb, h0v=h0v: e.dma_start(out=HSD[0, :, :, hb:hb + 1], in_=h0v[:, 0, :].unsqueeze(2)), r=[h0res], w=['HSD'], dsem='stH0')
            P.op('sync', lambda e, hb=hb, n=n, h0v=h0v: e.dma_start(out=HSD[1, :, :, hb + n:hb + n + 1], in_=h0v[:, 1, :].unsqueeze(2)), r=[h0res], w=['HSD'], dsem='stH0')
            for bi in range(nb):
                kf = c0 + bi * kc
                kbk = c0 + (nb - 1 - bi) * kc
                P.op('sync', lambda e, kf=kf, kc=kc: e.dma_start(out=UGF[0][:, :, 0:kc], in_=UG[:, :, kf:kf + kc].rearrange("g p k -> p g k")),
                     r=['UGd'], w=['UGF0'], dsem='ldUGF0')
                P.op('sync', lambda e, kbk=kbk, kc=kc: e.dma_start(out=UGF[1][:, :, 0:kc], in_=UG[:, :, kbk:kbk + kc].rearrange("g p k -> p g k")),
                     r=['UGd'], w=['UGF1'], dsem='ldUGF1')
                for d in range(2):
                    for r_ in range(2):
                        for q in range(4):
                            b = nextbank()
                            for g8 in range(8):
                                g32 = q * 8 + g8
                                for gh in range(2):
                                    g = gh * 32 + g32
                                    P.op('tensor', lambda e, d=d, r_=r_, g=g, gh=gh, g8=g8, b=b, kc=kc: e.matmul(
                                        ps[gh * 64:(gh + 1) * 64, b, g8 * kc:(g8 + 1) * kc], lhsT=QT[:, d, g, r_, :], rhs=UGF[d][:, g, 0:kc],
                                        start=True, stop=True), r=['QT', 'UGF%d' % d], w=['ps%d' % b])
                            src = ps[:, b, 0:8 * kc].rearrange("p (g k) -> p g k", g=8)
                            dst = SB_[:, d, r_, q * 8:(q + 1) * 8, 0:kc]
                            if d == 1:
                                dst = dst[:, :, ::-1]
                            P.op('scalar', lambda e, src=src, dst=dst: e.activation(out=dst, in_=src, func=AF.Copy), r=['ps%d' % b], w=['SB'])
                def hd(x, dd):
                    return x.rearrange("p (d q) -> p d q", d=2)[:, dd, :]
                for kq in range(kc):
                    first = (kq == 0)
                    for dd in range(2):
                        sbk = SB_[:, dd, :, :, kq].rearrange("p r g -> p (r g)")
                        if kq == 0 and bi == 0:
                            hprev = hd(Hs, dd)
                        elif kq == 0:
                            hprev = HBF[:, dd, :, :, kc - 1].rearrange("p r g -> p (r g)")
                        else:
                            hprev = HBF[:, dd, :, :, kq - 1].rearrange("p r g -> p (r g)")
                        P.op('vector', lambda e, sbk=sbk, hprev=hprev, dd=dd: e.tensor_tensor(out=hd(Zs, dd), in0=hprev, in1=sbk, op=ALU.add),
                             r=['Hs', 'SB', 'HBF'], w=['Zs'], skip_self=not first)
                    for dd in range(2):
                        zb = hd(Zs, dd).rearrange("p (r g) -> p r g", r=2).unsqueeze(1).broadcast_to([128, 2, 2, 32])
                        P.op('vector', lambda e, dd=dd, zb=zb: e.tensor_tensor(out=TT4[:, dd], in0=zb, in1=WW4[:, dd], op=ALU.mult),
                             r=['Zs', 'WW'], w=['TT'], skip_self=not first)
                    for dd in range(2):
                        hout = HBF[:, dd, :, :, kq]
                        P.op('vector', lambda e, hout=hout, dd=dd: e.tensor_tensor(out=hout, in0=TT4[:, dd, 0], in1=TT4[:, dd, 1][:, ::-1, :], op=ALU.add),
                             r=['TT'] + (['HSB'] if kq == 0 else []), w=['HBF'], skip_self=not first)
                P.op('scalar', lambda e, kc=kc: e.activation(out=HSB[:, 0, :, :, 0:kc], in_=HBF[:, 0, :, :, 0:kc], func=AF.Copy), r=['HBF'], w=['HSB'])
                P.op('scalar', lambda e, kc=kc: e.activation(out=HSB[:, 1, :, :, 0:kc][:, :, :, ::-1], in_=HBF[:, 1, :, :, 0:kc], func=AF.Copy),
                     r=['HBF'], w=['HSB'])
                P.op('sync', lambda e, hb=hb, kf=kf, c0=c0, kc=kc: e.dma_start(
                    out=HSD[0, :, :, hb + (kf - c0) + 1: hb + (kf - c0) + 1 + kc], in_=HSB[:, 0, :, :, 0:kc].rearrange("p r g k -> p (r g) k")),
                    r=['HSB'], w=['HSD'], dsem='stHS')
                P.op('sync', lambda e, hb=hb, kbk=kbk, c0=c0, kc=kc: e.dma_start(
                    out=HSD[1, :, :, hb + (kbk - c0): hb + (kbk - c0) + kc], in_=HSB[:, 1, :, :, 0:kc].rearrange("p r g k -> p (r g) k")),
                    r=['HSB'], w=['HSD'], dsem='stHS')
            if si > 0:
                hv4 = HBF[:, :, :, :, kc - 1]
                for gh in range(2):
                    for ri, dstt in enumerate([ns5re, ns5im]):
                        for d in range(2):
                            P.op('sync', lambda e, gh=gh, ri=ri, dstt=dstt, si=si, hv4=hv4, d=d: e.dma_start(
                                out=dstt[si - 1, d, gh * 32:(gh + 1) * 32, :].rearrange("g p -> p g"), in_=hv4[gh * 64:(gh + 1) * 64, d, ri, :]),
                                r=['HBF'], w=['ns5'], dsem='stout')
        P.barrier()

        yb = Bump(d_base, NW)
        KY = 256
        UGY = yb.bf(64 * KY).rearrange("p (g k) -> p g k", g=64)
        HSF = yb.bf(64 * KY).rearrange("p (r g k) -> p r g k", r=2, g=32)
        HSG = yb.bf(64 * KY).rearrange("p (r g k) -> p r g k", r=2, g=32)
        for si, (c0, n, hb) in enumerate(SEQC):
            kc = min(KY, n)
            gpb = 512 // kc if kc >= 64 else 8
            gpb = min(gpb, 8)
            for bi in range(n // kc):
                k0 = c0 + bi * kc
                col = hb + bi * kc
                P.op('sync', lambda e, k0=k0, kc=kc: e.dma_start(out=UGY[:, :, 0:kc], in_=UG[:, :, k0:k0 + kc].rearrange("g p k -> p g k")),
                     r=['UGd'], w=['UGY%d' % q for q in range(64)], dsem='ldUGY')
                P.op('sync', lambda e, col=col, kc=kc: e.dma_start(out=HSF[:, :, :, 0:kc].rearrange("p r g k -> p (r g) k"), in_=HSD[0, :, :, col:col + kc]),
                     w=['HSF'], dsem='ldHSF')
                P.op('sync', lambda e, col=col, kc=kc: e.dma_start(out=HSG[:, :, :, 0:kc].rearrange("p r g k -> p (r g) k"), in_=HSD[1, :, :, col + 1:col + 1 + kc]),
                     w=['HSG'], dsem='ldHSG')
                for gb in range(64 // gpb):
                    b = nextbank()
                    gl = [gb * gpb + gi for gi in range(gpb)]
                    ures = ['UGY%d' % g for g in gl]
                    for gi, g in enumerate(gl):
                        o = ps[:, b, gi * kc:(gi + 1) * kc]
                        P.op('tensor', lambda e, o=o, g=g, kc=kc, gi=gi: e.matmul(o, lhsT=MT[:, g, :], rhs=UGY[:, g, 0:kc], start=(gi == 0), stop=False),
                             r=ures + ['HSF', 'HSG'], w=['ps%d' % b])
                    for gi, g in enumerate(gl):
                        gh, g32 = g // 32, g % 32
                        psl = slice(gh * 64, (gh + 1) * 64)
                        o = ps[:, b, gi * kc:(gi + 1) * kc]
                        for d, hsrc in ((0, HSF), (1, HSG)):
                            for r_ in range(2):
                                last = (gi == gpb - 1 and d == 1 and r_ == 1)
                                P.op('tensor', lambda e, o=o, d=d, r_=r_, g32=g32, psl=psl, hsrc=hsrc, kc=kc, last=last: e.matmul(
                                    o, lhsT=PT[psl, d, g32, r_, :], rhs=hsrc[psl, r_, g32, 0:kc], start=False, stop=last),
                                    w=['ps%d' % b])
                    P.op('scalar', lambda e, b=b, gl=gl, kc=kc, gpb=gpb: e.activation(
                        out=UGY[:, gl[0]:gl[0] + gpb, 0:kc], in_=ps[:, b, 0:gpb * kc].rearrange("p (g k) -> p g k", g=gpb), func=AF.Gelu_apprx_tanh),
                        r=['ps%d' % b], w=ures)
                P.op('sync', lambda e, k0=k0, kc=kc: e.dma_start(out=UG[:, :, k0:k0 + kc].rearrange("g p k -> p g k"), in_=UGY[:, :, 0:kc]),
                     r=['UGY%d' % q for q in range(64)], w=['UGz'], dsem='stUGz')
        keep = {k: v for k, v in P.lastw.items() if k.startswith('WUP') or k.startswith('Xd') or k == 'UGz'}
        P.barrier()
        P.lastw.update(keep)

        ZG = WU[0]
        ZGb = CV[0]
        ZGT = TMP
        SGm = BCB
        Ztm = AXb[:, 0:8 * D].rearrange("p (j c) -> p j c", j=8)
        ZGL = H2
        def load_zgl(t_):
            _, np2 = TT[t_]
            kb_ = t_ * 128 if t_ < 4 else 512
            zg_ = H2.rearrange("p c n -> p (c n)")[:, 0:64 * np2].rearrange("p (g k) -> p g k", g=64)
            P.op('sync', lambda e: e.dma_start(out=zg_, in_=UG[:, :, kb_:kb_ + np2].rearrange("g p k -> p g k")),
                 r=['UGz'], w=['H2_%d' % ci for ci in range(8)], dsem='ldZT')

        for t in range(5):
            base, np_ = TT[t]
            g = 0 if t < 4 else 1
            ntok = 8 * np_
            kb = t * 128 if t < 4 else 512
            if t == 0 or t == 4:
                load_bc(GT1, 1, g, 0, 128, 'GT1')
                load_bc(GT2, 1, g, 1, 128, 'GT2')
                load_wd(1, np_)
                load_wmix(1)
                if t == 0:
                    P.op('sync', lambda e: e.dma_start(out=BCA, in_=norm_final[0:1, :].broadcast_to([128, D])), w=['BCA'], dsem='ldBCA')
            P.op('sync', lambda e, t=t, np_=np_: e.dma_start(out=XT[0:np_], in_=xtile_dram(t, scr=True)), r=['Xd%d' % t], w=XTres, dsem='ldXT')
            zgl = H2.rearrange("p c n -> p (c n)")[:, 0:64 * np_].rearrange("p (g k) -> p g k", g=64)
            if t == 0:
                load_zgl(0)
            for g8 in range(8):
                b = nextbank()
                for gi in range(8):
                    P.op('tensor', lambda e, g8=g8, gi=gi, b=b, np_=np_, zgl=zgl: e.transpose(
                        out=psbf(b)[0:np_, gi * 128:(gi + 1) * 128], in_=zgl[:, g8 * 8 + gi, :], identity=ident),
                        r=['H2_%d' % ci for ci in range(8)] + ['ident'], w=['ps%d' % b])
                srcv = psbf(b)[0:np_, :].rearrange("p (g j c) -> p g j c", g=8, j=8)
                dstv = Ztm[0:np_, :, g8 * 128:(g8 + 1) * 128].rearrange("p j (g c) -> p g j c", g=8)
                if g8 % 2 == 0:
                    P.op('scalar', lambda e, srcv=srcv, dstv=dstv: e.activation(out=dstv, in_=srcv, func=AF.Copy),
                         r=['ps%d' % b], w=['cXN%d' % j for j in range(8)])
                else:
                    P.op('vector', lambda e, srcv=srcv, dstv=dstv: e.tensor_copy(out=dstv, in_=srcv),
                         r=['ps%d' % b], w=['cXN%d' % j for j in range(8)])
            to_fm(Ztm, np_, lambda ci, ntok=ntok: H2[:, ci, 0:ntok], 'cXN', 'H2_')
            H2res = ['H2_%d' % ci for ci in range(8)]
            for j in range(8):
                b0 = nextbank(4)
                for n in range(4):
                    for ci in range(8):
                        P.op('tensor', lambda e, ci=ci, j=j, n=n, b=b0 + n, ntok=ntok, np_=np_: e.matmul(
                            ps[0:np_, b, :], lhsT=H2[:, ci, j:ntok:8], rhs=WMIX[:, ci, n * 512:(n + 1) * 512], start=(ci == 0), stop=(ci == 7)),
                            r=['WMIX'] + (H2res if ci == 0 else []), w=['ps%d' % (b0 + n)])
                pvv = psb(b0, 2); pvg = psb(b0 + 2, 2)
                P.op('scalar', lambda e, pvg=pvg, np_=np_: e.activation(out=SGm[0:np_, :], in_=pvg[0:np_, :], func=AF.Sigmoid),
                     r=['ps%d' % (b0 + 2), 'ps%d' % (b0 + 3)], w=['SGm'])
                P.op('vector', lambda e, pvv=pvv, np_=np_: e.tensor_tensor(out=TMP[0:np_, :], in0=pvv[0:np_, :], in1=SGm[0:np_, :], op=ALU.mult),
                     r=['ps%d' % b0, 'ps%d' % (b0 + 1), 'SGm'], w=['TMP'])
                P.op('vector', lambda e, j=j, np_=np_: e.tensor_tensor(out=XT[0:np_, j, :], in0=XT[0:np_, j, :], in1=TMP[0:np_, :], op=ALU.add),
                     r=['TMP', 'cXT%d' % j], w=['cXT%d' % j])
            ffn(1, t, np_, g)
            if t < 4:
                load_zgl(t + 1)
            rms_rstd(XT, np_, rstd, JUNK1, 'c', jres=lambda j: 'UGS1')
            for j in range(8):
                P.op('vector', lambda e, j=j, np_=np_: e.scalar_tensor_tensor(out=XT[0:np_, j, :], in0=XT[0:np_, j, :], scalar=rstd[0:np_, j:j + 1],
                                                                         in1=BCA[0:np_, :], op0=ALU.mult, op1=ALU.mult),
                     r=['cXT%d' % j, 'rstd', 'BCA'], w=['cXT%d' % j])
            P.op('sync', lambda e, t=t, np_=np_: e.dma_start(out=xtile_dram(t, for_out=True), in_=XT[0:np_]), r=XTres, w=['Xd%d' % t], dsem='stX')

        return finish()


def _in_maps(inp):
    f = lambda a: np.ascontiguousarray(np.asarray(a, dtype=np.float32))
    maps = []
    for b in range(8):
        m = {
            "xs": f(inp["x_sample"][b]),
            "xp": f(inp["x_prompt"][2 * b:2 * b + 2].reshape(2 * LP, D)),
            "st_lru": f(inp["state_lru"][b, 0]),
            "st_re": f(inp["state_s5_re"][b, 0]),
            "st_im": f(inp["state_s5_im"][b, 0]),
            "cvec": f(np.stack([np.asarray(inp["c"])[b], np.asarray(inp["c_ctx"])], 0)),
            "ada_w": f(inp["ada_w"]), "ada_b": f(inp["ada_b"]),
            "norm_mix": f(inp["norm_mix"]), "norm_ffn": f(inp["norm_ffn"]), "norm_final": f(np.asarray(inp["norm_final"])[None, :]),
            "lru_w_in": f(inp["lru_w_in"][0]), "lru_conv_w": f(inp["lru_conv_w"][0]), "lru_conv_b": f(np.asarray(inp["lru_conv_b"])[0][None, :]),
            "lru_w_a": f(inp["lru_w_a"][0]), "lru_b_a": f(inp["lru_b_a"][0]),
            "lru_w_i": f(inp["lru_w_i"][0]), "lru_b_i": f(inp["lru_b_i"][0]),
            "lru_lambda": f(inp["lru_lambda"][0]), "lru_w_out": f(inp["lru_w_out"][0]),
            "s5_a_re": f(inp["s5_a_re"][0]), "s5_a_im": f(inp["s5_a_im"][0]), "s5_log_dt": f(inp["s5_log_dt"][0]),
            "s5_b_re": f(inp["s5_b_re"][0]), "s5_b_im": f(inp["s5_b_im"][0]),
            "s5_c_re": f(inp["s5_c_re"][0]), "s5_c_im": f(inp["s5_c_im"][0]),
            "s5_d": f(np.asarray(inp["s5_d"])[0][None, :]), "s5_w_glu": f(inp["s5_w_glu"][0]),
            "ffn_w_up": f(inp["ffn_w_up"]), "ffn_conv_w": f(inp["ffn_conv_w"]), "ffn_conv_b": f(inp["ffn_conv_b"]),
            "ffn_w_down": f(inp["ffn_w_down"]),
        }
        maps.append(m)
    return maps


def run(inp, stop_after='E', trace=False):
    nc = build(stop_after)
    res = run_bass_kernel_spmd(nc, _in_maps(inp), core_ids=list(range(8)), trace=trace)
    return res


def kernel(**inp):
    res = run(inp)
    r = res.results
    y_prompt = np.concatenate([r[b]["yp"].reshape(2, LP, D) for b in range(8)], 0)
    y_sample = np.stack([r[b]["ys"] for b in range(8)], 0)
    nl = np.concatenate([r[b]["nlru"].reshape(2, 1, 2, D) for b in range(8)], 0)
    nre = np.concatenate([r[b]["ns5re"].reshape(2, 1, 2, 64, 64) for b in range(8)], 0)
    nim = np.concatenate([r[b]["ns5im"].reshape(2, 1, 2, 64, 64) for b in range(8)], 0)
    return (y_prompt.astype(np.float32), y_sample.astype(np.float32), nl.astype(np.float32),
            nre.astype(np.float32), nim.astype(np.float32))
```

```python
import contextlib
import numpy as np
import concourse.bass as bass
import concourse.mybir as mybir
from concourse.bass_utils import run_bass_kernel_spmd

F32 = mybir.dt.float32
BF16 = mybir.dt.bfloat16
AF = mybir.ActivationFunctionType
ALU = mybir.AluOpType
AX = mybir.AxisListType

D = 1024
NT = 4608
LS = 4096
LP = 256
DFF = 2816
NF = 22
EPS = 1e-6
NCH = 576
SEQS = [(0, 4096), (4096, 256), (4352, 256)]
TT = [(0, 128), (1024, 128), (2048, 128), (3072, 128), (4096, 64)]
ENGS = ['sync', 'scalar', 'vector', 'gpsimd', 'tensor']


class Sched:
    def __init__(self, nc, stack):
        self.nc = nc
        self.stack = stack
        self.q = {e: [] for e in ENGS}
        self.cnt = {e: 0 for e in ENGS}
        self.waited = {e: {} for e in ENGS}
        self.lastw = {}
        self.readers = {}
        self.sems = {}
        self.dcount = {}
        self.pending = {e: {} for e in ENGS}
        for e in ENGS:
            self.sem('E' + e)

    def sem(self, key):
        if key not in self.sems:
            self.sems[key] = self.stack.enter_context(self.nc.semaphore('s' + str(len(self.sems))))
            self.dcount[key] = 0
        return self.sems[key]

    def op(self, eng, fn, r=(), w=(), dsem=None, skip_self=False):
        deps = dict(self.pending[eng])
        self.pending[eng] = {}

        def add(tok):
            k, v, te = tok
            if te == 'tensor' and eng == 'tensor':
                return
            if skip_self and te == eng:
                return
            if deps.get(k, 0) < v:
                deps[k] = v
        for x in r:
            if x in self.lastw:
                add(self.lastw[x])
        for x in w:
            if x in self.lastw:
                add(self.lastw[x])
            for t in self.readers.get(x, ()):
                add(t)
        waits = []
        for k, v in deps.items():
            if self.waited[eng].get(k, 0) < v:
                self.waited[eng][k] = v
                waits.append((k, v))
        if dsem is None:
            self.cnt[eng] += 1
            tok = ('E' + eng, self.cnt[eng], eng)
            inc = 1
        else:
            self.sem(dsem)
            self.dcount[dsem] += 16
            tok = (dsem, self.dcount[dsem], 'dma')
            inc = 16
        self.q[eng].append((waits, fn, tok[0], inc))
        for x in r:
            self.readers.setdefault(x, []).append(tok)
        for x in w:
            self.lastw[x] = tok
            self.readers[x] = []
        return tok

    def barrier(self):
        allt = {}
        for e in ENGS:
            if self.cnt[e]:
                allt['E' + e] = self.cnt[e]
        for k, v in self.dcount.items():
            if not k.startswith('E') and v:
                allt[k] = v
        for e in ENGS:
            for k, v in allt.items():
                if self.pending[e].get(k, 0) < v:
                    self.pending[e][k] = v
        self.lastw = {}
        self.readers = {}

    def emit(self, block):
        nc = self.nc

        def mk(ename):
            def body(e):
                for waits, fn, semk, inc in self.q[ename]:
                    for k, v in waits:
                        e.wait_ge(self.sems[k], v)
                    fn(e).then_inc(self.sems[semk], inc)
                for k, v in self.final.items():
                    e.wait_ge(self.sems[k], v)
            return body
        self.final = {}
        for e in ENGS:
            if self.cnt[e]:
                self.final['E' + e] = self.cnt[e]
        for k, v in self.dcount.items():
            if not k.startswith('E') and v:
                self.final[k] = v
        block.sync(mk('sync'))
        block.scalar(mk('scalar'))
        block.vector(mk('vector'))
        block.gpsimd(mk('gpsimd'))
        block.tensor(mk('tensor'))


def build(stop_after='E'):
    nc = bass.Bass("TRN2", target_bir_lowering=False)
    I = {}

    def din(name, shape, dt=F32):
        I[name] = nc.dram_tensor(name, list(shape), dt, kind="ExternalInput").ap()
        return I[name]

    def dout(name, shape, dt=F32):
        return nc.dram_tensor(name, list(shape), dt, kind="ExternalOutput").ap()

    def dscr(name, shape, dt):
        return nc.dram_tensor(name, list(shape), dt, kind="Internal").ap()

    xs = din("xs", [LS, D]); xp = din("xp", [2 * LP, D])
    st_lru = din("st_lru", [2, D]); st_re = din("st_re", [2, 64, 64]); st_im = din("st_im", [2, 64, 64])
    cvec = din("cvec", [2, D])
    ada_w = din("ada_w", [2, D, 6 * D]); ada_b = din("ada_b", [2, 6 * D])
    norm_mix = din("norm_mix", [2, D]); norm_ffn = din("norm_ffn", [2, D]); norm_final = din("norm_final", [1, D])
    lru_w_in = din("lru_w_in", [D, 2 * D]); lru_conv_w = din("lru_conv_w", [4, D]); lru_conv_b = din("lru_conv_b", [1, D])
    lru_w_a = din("lru_w_a", [2, 16, 64, 64]); lru_b_a = din("lru_b_a", [2, D])
    lru_w_i = din("lru_w_i", [2, 16, 64, 64]); lru_b_i = din("lru_b_i", [2, D])
    lru_lambda = din("lru_lambda", [2, D]); lru_w_out = din("lru_w_out", [D, D])
    s5_a_re = din("s5_a_re", [2, 64, 64]); s5_a_im = din("s5_a_im", [2, 64, 64]); s5_log_dt = din("s5_log_dt", [2, 64])
    s5_b_re = din("s5_b_re", [2, 64, 64, 16]); s5_b_im = din("s5_b_im", [2, 64, 64, 16])
    s5_c_re = din("s5_c_re", [2, 64, 16, 64]); s5_c_im = din("s5_c_im", [2, 64, 16, 64])
    s5_d = din("s5_d", [1, D]); s5_w_glu = din("s5_w_glu", [D, 2 * D])
    ffn_w_up = din("ffn_w_up", [2, D, 2 * DFF]); ffn_conv_w = din("ffn_conv_w", [2, 3, 2 * DFF])
    ffn_conv_b = din("ffn_conv_b", [2, 2 * DFF]); ffn_w_down = din("ffn_w_down", [2, DFF, D])

    ys = dout("ys", [LS, D]); yp = dout("yp", [2 * LP, D])
    nlru = dout("nlru", [2, 2, D]); ns5re = dout("ns5re", [2, 2, 64, 64]); ns5im = dout("ns5im", [2, 2, 64, 64])

    MODX = dscr("MODX", [2, 2, 6, D], F32)
    Z0 = dscr("Z0", [8, 128, NT], BF16)
    WUP = dscr("WUP", [2, NF, 128, 8, 2, 128], BF16)
    UG = dscr("UG", [64, 128, NCH], BF16)
    HSD = dscr("HSD", [2, 128, 64, NCH + 3], BF16)
    dbg = {}
    if stop_after != 'E':
        dbg['xs1'] = dout("dbg_xs", [LS, D]); dbg['xp1'] = dout("dbg_xp", [2 * LP, D])

    stack = contextlib.ExitStack()
    with stack:
        NW = 52200
        big = stack.enter_context(nc.sbuf_tensor("big", [128, NW], F32))
        ps = stack.enter_context(nc.psum_tensor("ps", [128, 8, 512], F32))
        P = Sched(nc, stack)
        stack.enter_context(nc.allow_non_contiguous_dma(reason="small strided parameter loads"))

        class Bump:
            def __init__(s, base, end):
                s.base = base; s.cur = base; s.end = end

            def f32(s, n):
                a = s.cur; s.cur += n
                assert s.cur <= s.end, (s.cur, s.end)
                return big[:, a:a + n]

            def bf(s, n):
                w = (n + 1) // 2
                a = s.cur; s.cur += w
                assert s.cur <= s.end, (s.cur, s.end)
                return big[:, a:a + w].bitcast(BF16)

        def psb(b, nb=1):
            return ps[:, b:b + nb, :].rearrange("p b n -> p (b n)") if nb > 1 else ps[:, b, :]

        def psbf(b):
            return ps[:, b, :].bitcast(BF16)

        def dump(name, src, shape, dt=F32, r=()):
            o = dout("dbg_" + name, shape, dt)
            P.op('sync', lambda e: e.dma_start(out=o, in_=src), r=list(r), dsem='stdbg')

        def finish():
            with nc.Block() as block:
                P.emit(block)
            return nc

        pers = Bump(0, 1400)
        ident = pers.bf(128)
        identf = pers.f32(128)
        modcol = pers.f32(2 * 2 * 4 * 8).rearrange("p (l g k c) -> p l g k c", l=2, g=2, k=4)
        lconv = pers.f32(40).rearrange("p (t c) -> p t c", t=5)
        lba = pers.f32(32).rearrange("p (w d c) -> p w d c", w=2, d=2)
        lcl = pers.f32(32).rearrange("p (k d c) -> p k d c", k=2, d=2)
        fconv = pers.f32(2 * 4 * 44).rearrange("p (l t f) -> p l t f", l=2, t=4)
        lruout = pers.f32(32).rearrange("p (s d c) -> p s d c", s=2, d=2)
        small = pers.f32(64)
        PBASE = 1400

        P.op('gpsimd', lambda e: e.memset(identf, 0.0), w=['identf'])
        P.op('gpsimd', lambda e: e.affine_select(out=identf, in_=identf, pattern=[[-1, 128]], compare_op=ALU.not_equal,
                                                 fill=1.0, base=0, channel_multiplier=1), r=['identf'], w=['identf'])
        P.op('gpsimd', lambda e: e.tensor_copy(out=ident, in_=identf), r=['identf'], w=['ident'])

        wupconv = []
        for l in range(2):
            wv = ffn_w_up[l].rearrange("(ci p) (h f m) -> f p ci h m", p=128, h=2, m=128)
            for f in range(NF):
                for h in range(2):
                    wupconv.append((l, f, h, wv))

        def emit_wupconv(n):
            for _ in range(n):
                if not wupconv:
                    return
                l, f, h, wv = wupconv.pop(0)
                P.op('gpsimd', lambda e, l=l, f=f, h=h, wv=wv: e.dma_start(out=WUP[l, f, :, :, h, :], in_=wv[f, :, :, h, :]),
                     w=['WUP%d_%d_%d' % (l, f, h)], dsem='wupcv')

        def ldcol(dst, src, res, eng='sync'):
            P.op(eng, lambda e: e.dma_start(out=dst, in_=src), w=[res], dsem='ld' + res)
        for t in range(4):
            ldcol(lconv[:, t, :], lru_conv_w[t].rearrange("(c p) -> p c", p=128), 'lconv')
        ldcol(lconv[:, 4, :], lru_conv_b[0].rearrange("(c p) -> p c", p=128), 'lconv')
        for d in range(2):
            ldcol(lba[:, 0, d, :], lru_b_a[d].rearrange("(c p) -> p c", p=128), 'lba')
            ldcol(lba[:, 1, d, :], lru_b_i[d].rearrange("(c p) -> p c", p=128), 'lba')
            ldcol(lcl[:, 0, d, :], lru_lambda[d].rearrange("(c p) -> p c", p=128), 'lcl')
            ldcol(lcl[:, 1, d, :], st_lru[d].rearrange("(c p) -> p c", p=128), 'lcl')
        for l in range(2):
            for t in range(3):
                ldcol(fconv[:, l, t, :], ffn_conv_w[l, t].rearrange("(f p) -> p f", p=128), 'fconv')
            ldcol(fconv[:, l, 3, :], ffn_conv_b[l].rearrange("(f p) -> p f", p=128), 'fconv')
        clv = lcl[:, 0, :, :]
        P.op('scalar', lambda e: e.activation(out=clv, in_=clv, func=AF.Exp, scale=-1.0), r=['lcl'], w=['lcl'])
        P.op('scalar', lambda e: e.activation(out=clv, in_=clv, func=AF.Ln, bias=1.0, scale=1.0), r=['lcl'], w=['lcl'])
        P.op('vector', lambda e: e.tensor_scalar(out=clv, in0=clv, scalar1=-8.0, scalar2=None, op0=ALU.mult), r=['lcl'], w=['lcl'])

        sb = Bump(PBASE, NW)
        cvT = sb.f32(16).rearrange("p (c g) -> p c g", g=2)
        adaw = [sb.f32(8 * 512).rearrange("p (c n) -> p c n", c=8) for _ in range(2)]
        modrow = sb.f32(6 * D)
        nrow = sb.f32(3 * D)
        for g in range(2):
            P.op('sync', lambda e, g=g: e.dma_start(out=cvT[:, :, g], in_=cvec[g].rearrange("(c p) -> p c", p=128)), w=['cvT'], dsem='ldcvT')
        cvf = cvT.rearrange("p c g -> p (c g)")
        P.op('scalar', lambda e: e.activation(out=cvf, in_=cvf, func=AF.Silu), r=['cvT'], w=['cvT'])
        for l in range(2):
            P.op('sync', lambda e, l=l: e.dma_start(out=modrow[0:2, :], in_=ada_b[l:l + 1, :].broadcast_to([2, 6 * D])), w=['modrow'], dsem='ldmr0')
            P.op('sync', lambda e, l=l: e.dma_start(out=nrow[0:2, 0:D], in_=norm_mix[l:l + 1, :].broadcast_to([2, D])), w=['nrow'], dsem='ldmr1')
            P.op('sync', lambda e, l=l: e.dma_start(out=nrow[0:2, D:2 * D], in_=norm_ffn[l:l + 1, :].broadcast_to([2, D])), w=['nrow'], dsem='ldmr1')
            for pc in range(12):
                buf = adaw[pc % 2]
                P.op('sync', lambda e, l=l, pc=pc, buf=buf: e.dma_start(
                    out=buf, in_=ada_w[l, :, pc * 512:(pc + 1) * 512].rearrange("(c p) n -> p c n", p=128)),
                    w=['adaw%d' % (pc % 2)], dsem='adaw%d' % (pc % 2))
                bk = pc % 2
                for ci in range(8):
                    P.op('tensor', lambda e, ci=ci, buf=buf, bk=bk: e.matmul(ps[0:2, bk, :], lhsT=cvT[:, ci, :], rhs=buf[:, ci, :],
                                                                           start=(ci == 0), stop=(ci == 7)),
                         r=['cvT', 'adaw%d' % (pc % 2)], w=['ps%d' % bk])
                mr = modrow[0:2, pc * 512:(pc + 1) * 512]
                P.op('vector', lambda e, mr=mr, bk=bk: e.tensor_tensor(out=mr, in0=mr, in1=ps[0:2, bk, :], op=ALU.add),
                     r=['ps%d' % bk, 'modrow'], w=['modrow'])
            def mrc(i):
                return modrow[0:2, i * D:(i + 1) * D]
            P.op('vector', lambda e: e.scalar_tensor_tensor(out=mrc(1), in0=mrc(1), scalar=1.0, in1=nrow[0:2, 0:D], op0=ALU.add, op1=ALU.mult),
                 r=['modrow', 'nrow'], w=['modrow'])
            P.op('vector', lambda e: e.scalar_tensor_tensor(out=mrc(4), in0=mrc(4), scalar=1.0, in1=nrow[0:2, D:2 * D], op0=ALU.add, op1=ALU.mult),
                 r=['modrow', 'nrow'], w=['modrow'])
            for kind, ch in enumerate([2, 5, 1, 0, 4, 3]):
                P.op('sync', lambda e, l=l, kind=kind, ch=ch: e.dma_start(out=MODX[l, :, kind, :], in_=modrow[0:2, ch * D:(ch + 1) * D]),
                     r=['modrow'], w=['MODX'], dsem='stmodx')
        for l in range(2):
            for g in range(2):
                for k in range(4):
                    P.op('sync', lambda e, l=l, g=g, k=k: e.dma_start(out=modcol[:, l, g, k, :], in_=MODX[l, g, 2 + k].rearrange("(c p) -> p c", p=128)),
                         r=['MODX'], w=['modcol'], dsem='ldsmall')
        P.barrier()

        def load_bc(dst, l, g, kind, np_, res):
            P.op('sync', lambda e: e.dma_start(out=dst[0:np_, :], in_=MODX[l, g, kind:kind + 1, :].broadcast_to([np_, D])),
                 r=['MODX'], w=[res], dsem='ld' + res)

        def xtile_dram(t, for_out=False, scr=False):
            base, np_ = TT[t]
            if t < 4:
                src = (ys if (for_out or scr) else xs)[base:base + 1024, :]
            else:
                src = (yp if (for_out or scr) else xp)[:, :]
            return src.rearrange("(k j) c -> k j c", j=8)

        def rms_rstd(XT, np_, rstd, junk, tag, jres=None):
            ssq = small[:, 0:8]
            for j in range(8):
                P.op('scalar', lambda e, j=j: e.activation(out=junk[0:np_, j, :], in_=XT[0:np_, j, :], func=AF.Square, accum_out=ssq[0:np_, j:j + 1]),
                     r=[tag + 'XT%d' % j], w=['ssq', (jres(j) if jres else tag + 'XN%d' % j)])
            P.op('vector', lambda e: e.tensor_scalar(out=rstd[0:np_, :], in0=ssq[0:np_, :], scalar1=1.0 / D, scalar2=EPS, op0=ALU.mult, op1=ALU.add),
                 r=['ssq'], w=['rstd'])
            P.op('scalar', lambda e: e.activation(out=rstd[0:np_, :], in_=rstd[0:np_, :], func=AF.Sqrt), r=['rstd'], w=['rstd'])
            P.op('vector', lambda e: e.reciprocal(out=rstd[0:np_, :], in_=rstd[0:np_, :]), r=['rstd'], w=['rstd'])

        bankrr = [0]

        def nextbank(n=1):
            b = bankrr[0]
            if b + n > 8:
                b = 0
            bankrr[0] = (b + n) % 8
            return b

        def to_fm(SRC, np_, dst_fn, rtag, wtag, scale_col=None, bias_col=None):
            for ci in range(8):
                b = nextbank()
                pv = psbf(b)[:, 0:8 * np_].rearrange("p (j k) -> p j k", j=8)
                for j in range(8):
                    P.op('tensor', lambda e, ci=ci, j=j, pv=pv: e.transpose(out=pv[:, j, :], in_=SRC[0:np_, j, ci * 128:(ci + 1) * 128],
                                                                            identity=ident[0:np_, 0:np_]),
                         r=[rtag + '%d' % j, 'ident'], w=['ps%d' % b])
                dst = dst_fn(ci).rearrange("p (k j) -> p j k", j=8)
                if scale_col is not None:
                    if ci % 2 == 0:
                        P.op('scalar', lambda e, ci=ci, pv=pv, dst=dst: e.activation(out=dst, in_=pv, func=AF.Identity,
                                                                                   scale=scale_col[:, ci:ci + 1], bias=bias_col[:, ci:ci + 1]),
                             r=['ps%d' % b, 'modcol'], w=[wtag + '%d' % ci])
                    else:
                        P.op('vector', lambda e, ci=ci, pv=pv, dst=dst: e.tensor_scalar(out=dst, in0=pv, scalar1=scale_col[:, ci:ci + 1],
                                                                                      scalar2=bias_col[:, ci:ci + 1], op0=ALU.mult, op1=ALU.add),
                             r=['ps%d' % b, 'modcol'], w=[wtag + '%d' % ci])
                else:
                    if ci % 2 == 0:
                        P.op('scalar', lambda e, pv=pv, dst=dst: e.activation(out=dst, in_=pv, func=AF.Copy), r=['ps%d' % b], w=[wtag + '%d' % ci])
                    else:
                        P.op('vector', lambda e, pv=pv, dst=dst: e.tensor_copy(out=dst, in_=pv), r=['ps%d' % b], w=[wtag + '%d' % ci])

        ab = Bump(PBASE, NW)
        Hb = ab.bf(8 * NT).rearrange("p (c n) -> p c n", c=8)
        phB_base = ab.cur
        XTa = ab.f32(8 * D).rearrange("p (j c) -> p j c", j=8)
        XNa = ab.bf(8 * D).rearrange("p (j c) -> p j c", j=8)
        rstd = small[:, 8:16]
        for t in range(5):
            base, np_ = TT[t]
            g = 0 if t < 4 else 1
            P.op('sync', lambda e, t=t, np_=np_: e.dma_start(out=XTa[0:np_], in_=xtile_dram(t)),
                 w=['aXT%d' % j for j in range(8)], dsem='ldXTa')
            rms_rstd(XTa, np_, rstd, XNa, 'a')
            for j in range(8):
                if j % 2 == 0:
                    P.op('vector', lambda e, j=j, np_=np_: e.tensor_scalar(out=XNa[0:np_, j, :], in0=XTa[0:np_, j, :], scalar1=rstd[0:np_, j:j + 1],
                                                                           scalar2=None, op0=ALU.mult),
                         r=['aXT%d' % j, 'rstd', 'aXN%d' % j], w=['aXN%d' % j])
                else:
                    P.op('scalar', lambda e, j=j, np_=np_: e.activation(out=XNa[0:np_, j, :], in_=XTa[0:np_, j, :], func=AF.Copy, scale=rstd[0:np_, j:j + 1]),
                         r=['aXT%d' % j, 'rstd', 'aXN%d' % j], w=['aXN%d' % j])
            to_fm(XNa, np_, lambda ci, base=base, np_=np_: Hb[:, ci, base:base + 8 * np_], 'aXN', 'H_t%d_' % t,
                  scale_col=modcol[:, 0, g, 0, :], bias_col=modcol[:, 0, g, 1, :])
        Hres = []
        P.barrier()
        if stop_after == 'C1':
            dump('H_A', Hb, [128, 8, NT], BF16)
        if stop_after == 'A':
            dump('modx', MODX, [2, 2, 6, D])
            dump('modcol', modcol.rearrange("p l g k c -> p (l g k c)"), [128, 128])
            dump('lcl', lcl.rearrange("p k d c -> p (k d c)"), [128, 32])
            dump('H', Hb, [128, 8, NT], BF16)
            return finish()

        bb = Bump(phB_base, NW)
        WIN = [bb.bf(8 * 2 * 128).rearrange("p (c h m) -> p c h m", c=8, h=2) for _ in range(2)]
        WAI = bb.bf(8 * 2 * 2 * 128).rearrange("p (c w d m) -> p c w d m", c=8, w=2, d=2)
        WAIf = bb.f32(2 * 2 * 128).rearrange("p (w d m) -> p w d m", w=2, d=2)
        GG = bb.bf(NT)
        XC = bb.f32(NT)
        XCb = bb.bf(NT)
        RA = bb.f32(NT)
        IG = bb.f32(NT)
        T01 = [bb.f32(NT), bb.f32(NT)]
        NTL = [(i * 512, 512) for i in range(9)]

        ESEG = [(0, 2047), (2047, 4096), (4096, 4608)]

        def segs_of(lo, hi):
            return [i for i, (a_, b_) in enumerate(ESEG) if a_ < hi and b_ > lo]

        def rs(name, lo, hi):
            return ['%s_%d' % (name, i) for i in segs_of(lo, hi)]

        for c in range(8):
            wb = WIN[c % 2]
            for h in range(2):
                P.op('gpsimd', lambda e, c=c, h=h, wb=wb: e.dma_start(
                    out=wb[:, :, h, :], in_=lru_w_in[:, h * D + c * 128: h * D + (c + 1) * 128].rearrange("(ci p) m -> p ci m", p=128)),
                    w=['WIN%d' % (c % 2)], dsem='ldWIN%d' % (c % 2))
            P.op('gpsimd', lambda e: e.memset(WAIf.rearrange("p w d m -> p (w d m)"), 0.0), w=['WAIf%d' % i_ for i_ in range(8)])
            for wi, wsrc in enumerate([lru_w_a, lru_w_i]):
                for d in range(2):
                    for h2 in range(2):
                        P.op('sync', lambda e, wi=wi, d=d, h2=h2, wsrc=wsrc, c=c: e.dma_start(
                            out=WAIf[h2 * 64:(h2 + 1) * 64, wi, d, h2 * 64:(h2 + 1) * 64], in_=wsrc[d, 2 * c + h2]),
                            w=['WAIf%d' % (wi * 4 + d * 2 + h2)], dsem='ldWAI')
            P.op('gpsimd', lambda e, c=c: e.tensor_copy(out=WAI[:, c].rearrange("p w d m -> p (w d m)"), in_=WAIf.rearrange("p w d m -> p (w d m)")),
                 r=['WAIf%d' % i_ for i_ in range(8)], w=['WAI%d' % c])
            emit_wupconv(6)
            XR = IG
            for (n0, nn) in NTL:
                for h in range(2):
                    b = nextbank()
                    for ci in range(8):
                        P.op('tensor', lambda e, ci=ci, h=h, n0=n0, nn=nn, b=b, wb=wb: e.matmul(
                            ps[:, b, 0:nn], lhsT=wb[:, ci, h, :], rhs=Hb[:, ci, n0:n0 + nn], start=(ci == 0), stop=(ci == 7)),
                            r=['WIN%d' % (c % 2)], w=['ps%d' % b])
                    if h == 0:
                        P.op('scalar', lambda e, n0=n0, nn=nn, b=b: e.activation(out=GG[:, n0:n0 + nn], in_=ps[:, b, 0:nn], func=AF.Gelu_apprx_tanh),
                             r=['ps%d' % b], w=rs('GG', n0, n0 + nn))
                    else:
                        P.op('vector', lambda e, n0=n0, nn=nn, b=b: e.tensor_copy(out=IG[:, n0:n0 + nn], in_=ps[:, b, 0:nn]),
                             r=['ps%d' % b], w=rs('IG', n0, n0 + nn))
            for si_, (a_, b_) in enumerate(ESEG):
                seqs_in = [(s0, sl) for (s0, sl) in SEQS if s0 < b_ and s0 + sl > a_]
                rr_ = rs('IG', max(0, a_ - 2), min(NT, b_ + 1)) + ['lconv']
                P.op('vector', lambda e, c=c, a_=a_, b_=b_: e.tensor_scalar(out=XC[:, a_:b_], in0=XR[:, a_:b_], scalar1=lconv[:, 2, c:c + 1],
                                                                       scalar2=lconv[:, 4, c:c + 1], op0=ALU.mult, op1=ALU.add),
                     r=rr_, w=['XC_%d' % si_])
                for (s0, sl) in seqs_in:
                    for k, o in ((0, -2), (1, -1), (3, 1)):
                        lo = max(a_, s0 + max(0, -o)); hi = min(b_, s0 + sl - max(0, o))
                        P.op('vector', lambda e, c=c, k=k, o=o, lo=lo, hi=hi: e.scalar_tensor_tensor(
                            out=XC[:, lo:hi], in0=XR[:, lo + o:hi + o], scalar=lconv[:, k, c:c + 1], in1=XC[:, lo:hi], op0=ALU.mult, op1=ALU.add),
                            r=rr_ + ['XC_%d' % si_], w=['XC_%d' % si_])
            for si_, (a_, b_) in enumerate(ESEG):
                P.op('gpsimd', lambda e, a_=a_, b_=b_: e.tensor_copy(out=XCb[:, a_:b_], in_=XC[:, a_:b_]), r=['XC_%d' % si_], w=['XCb_%d' % si_])
            emit_wupconv(5)
            for d in range(2):
                T = T01[d]
                for (n0, nn) in NTL:
                    for wi in range(2):
                        dstbuf = RA if wi == 0 else IG
                        b = nextbank()
                        P.op('tensor', lambda e, wi=wi, d=d, n0=n0, nn=nn, b=b, c=c: e.matmul(
                            ps[:, b, 0:nn], lhsT=WAI[:, c, wi, d, :], rhs=XCb[:, n0:n0 + nn], start=True, stop=True),
                            r=['WAI%d' % c] + rs('XCb', n0, n0 + nn), w=['ps%d' % b])
                        P.op('scalar', lambda e, wi=wi, d=d, n0=n0, nn=nn, b=b, c=c, dstbuf=dstbuf: e.activation(
                            out=dstbuf[:, n0:n0 + nn], in_=ps[:, b, 0:nn], func=AF.Sigmoid, bias=lba[:, wi, d, c:c + 1], scale=1.0),
                            r=['ps%d' % b, 'lba'], w=rs('RA' if wi == 0 else 'IG', n0, n0 + nn))
                for si_, (a_, b_) in enumerate(ESEG):
                    P.op('scalar', lambda e, d=d, c=c, a_=a_, b_=b_: e.activation(out=RA[:, a_:b_], in_=RA[:, a_:b_], func=AF.Exp, scale=lcl[:, 0, d, c:c + 1]),
                         r=['RA_%d' % si_, 'lcl'], w=['RA_%d' % si_])
                for si_, (a_, b_) in enumerate(ESEG):
                    P.op('gpsimd', lambda e, T=T, a_=a_, b_=b_: e.tensor_tensor(out=T[:, a_:b_], in0=RA[:, a_:b_], in1=RA[:, a_:b_], op=ALU.mult),
                         r=['RA_%d' % si_], w=['T%d_%d' % (d, si_)])
                for si_, (a_, b_) in enumerate(ESEG):
                    P.op('scalar', lambda e, T=T, a_=a_, b_=b_: e.activation(out=T[:, a_:b_], in_=T[:, a_:b_], func=AF.Sqrt, scale=-1.0, bias=1.0),
                         r=['T%d_%d' % (d, si_)], w=['T%d_%d' % (d, si_)])
                for si_, (a_, b_) in enumerate(ESEG):
                    P.op('vector', lambda e, T=T, a_=a_, b_=b_: e.tensor_tensor(out=T[:, a_:b_], in0=T[:, a_:b_], in1=IG[:, a_:b_], op=ALU.mult),
                         r=['T%d_%d' % (d, si_), 'IG_%d' % si_], w=['T%d_%d' % (d, si_)])
                for si_, (a_, b_) in enumerate(ESEG):
                    P.op('gpsimd', lambda e, T=T, a_=a_, b_=b_: e.tensor_tensor(out=T[:, a_:b_], in0=T[:, a_:b_], in1=XC[:, a_:b_], op=ALU.mult),
                         r=['T%d_%d' % (d, si_), 'XC_%d' % si_], w=['T%d_%d' % (d, si_)])
                h0ap = lcl[:, 1, d, c:c + 1]
                if d == 0:
                    plan = [(0, 0, 2047, h0ap, None), (1, 2047, 4096, T[:, 2046:2047], 0), (2, 4096, 4352, 0.0, None), (2, 4352, 4608, 0.0, None)]
                else:
                    plan = [(1, 2047, 4096, h0ap, None), (0, 0, 2047, T[:, 2047:2048], 1), (2, 4096, 4352, 0.0, None), (2, 4352, 4608, 0.0, None)]
                for (si_, a_, b_, init, dep) in plan:
                    rr_ = ['RA_%d' % si_, 'T%d_%d' % (d, si_), 'lcl'] + (['T%d_%d' % (d, dep)] if dep is not None else [])
                    if d == 0:
                        P.op('vector', lambda e, T=T, a_=a_, b_=b_, init=init: e.tensor_tensor_scan(
                            out=T[:, a_:b_], data0=RA[:, a_:b_], data1=T[:, a_:b_], initial=init, op0=ALU.mult, op1=ALU.add),
                            r=rr_, w=['T%d_%d' % (d, si_)])
                    else:
                        P.op('vector', lambda e, T=T, a_=a_, b_=b_, init=init: e.tensor_tensor_scan(
                            out=T[:, a_:b_][:, ::-1], data0=RA[:, a_:b_][:, ::-1], data1=T[:, a_:b_][:, ::-1],
                            initial=init, op0=ALU.mult, op1=ALU.add),
                            r=rr_, w=['T%d_%d' % (d, si_)])
                for pi, (s0, sl) in enumerate(SEQS[1:]):
                    col = s0 + sl - 1 if d == 0 else s0
                    P.op('gpsimd', lambda e, T=T, col=col, pi=pi, d=d, c=c: e.tensor_copy(out=lruout[:, pi, d, c:c + 1], in_=T[:, col:col + 1]),
                         r=['T%d_2' % d], w=['lruout'])
            for si_, (a_, b_) in enumerate(ESEG):
                P.op('gpsimd', lambda e, a_=a_, b_=b_: e.tensor_tensor(out=T01[0][:, a_:b_], in0=T01[0][:, a_:b_], in1=T01[1][:, a_:b_], op=ALU.add),
                     r=['T0_%d' % si_, 'T1_%d' % si_], w=['T0_%d' % si_])
            for si_, (a_, b_) in enumerate(ESEG):
                P.op('vector', lambda e, a_=a_, b_=b_: e.tensor_tensor(out=XCb[:, a_:b_], in0=T01[0][:, a_:b_], in1=GG[:, a_:b_], op=ALU.mult),
                     r=['T0_%d' % si_, 'GG_%d' % si_, 'XCb_%d' % si_], w=['XCb_%d' % si_])
            P.op('sync', lambda e, c=c: e.dma_start(out=Z0[c], in_=XCb), r=['XCb_0', 'XCb_1', 'XCb_2'], w=['Z0_%d' % c], dsem='stZ0')
        for s in range(2):
            for d in range(2):
                P.op('sync', lambda e, s=s, d=d: e.dma_start(out=nlru[s, d].rearrange("(c p) -> p c", p=128), in_=lruout[:, s, d, :]),
                     r=['lruout'], w=['nlru'], dsem='stout')
        emit_wupconv(1000)
        if stop_after == 'C1':
            dump('H_B', Hb, [128, 8, NT], BF16)
        Z0res = ['Z0_%d' % c for c in range(8)]
        if stop_after == 'B':
            P.barrier()
            dump('Z0', Z0, [8, 128, NT], BF16)
            dump('GG', GG, [128, NT], BF16)
            dump('XC', XC, [128, NT])
            dump('RA', RA, [128, NT])
            dump('IG', IG, [128, NT])
            dump('T1', T01[1], [128, NT])
            return finish()
        z0tok = [P.lastw[r_] for r_ in Z0res]
        wuptok = {k: v for k, v in P.lastw.items() if k.startswith('WUP')}
        P.barrier()
        for r_, tk in zip(Z0res, z0tok):
            P.lastw[r_] = tk
        P.lastw.update(wuptok)

        cb = Bump(PBASE, NW)
        XT = cb.f32(8 * D).rearrange("p (j c) -> p j c", j=8)
        AXb = cb.bf(11 * 1024)
        ACTT = AXb.rearrange("p (f n) -> p f n", f=11)
        XN = AXb[:, 0:8 * D].rearrange("p (j c) -> p j c", j=8)

        def actres(fi):
            return ('cXN%d' if fi < 8 else 'ACT%d') % fi
        H2 = cb.bf(8 * 1024).rearrange("p (c n) -> p c n", c=8)
        CV = [cb.f32(1024) for _ in range(2)]
        CG = [cb.f32(1024) for _ in range(2)]
        WU = [cb.bf(8 * 2 * 128).rearrange("p (c h m) -> p c h m", c=8, h=2) for _ in range(3)]
        WD = cb.bf(NF * D).rearrange("p (f n) -> p f n", f=NF)
        GT1 = cb.f32(D); GT2 = cb.f32(D)
        TMP = cb.f32(D)
        WMIX = cb.bf(8 * 2048).rearrange("p (c n) -> p c n", c=8)
        BCA = cb.f32(D); BCB = cb.f32(D)
        UGS = [cb.bf(1024).rearrange("p (g k) -> p g k", g=8) for _ in range(2)]
        phCE_end = cb.cur
        wu_i = [0]

        def ffn(l, t, np_, g):
            ntok = 8 * np_
            rms_rstd(XT, np_, rstd, XN, 'c')
            for j in range(8):
                if j % 2 == 0:
                    P.op('vector', lambda e, j=j: e.tensor_scalar(out=XN[0:np_, j, :], in0=XT[0:np_, j, :], scalar1=rstd[0:np_, j:j + 1],
                                                                  scalar2=None, op0=ALU.mult),
                         r=['cXT%d' % j, 'rstd', 'cXN%d' % j], w=['cXN%d' % j])
                else:
                    P.op('scalar', lambda e, j=j: e.activation(out=XN[0:np_, j, :], in_=XT[0:np_, j, :], func=AF.Copy, scale=rstd[0:np_, j:j + 1]),
                         r=['cXT%d' % j, 'rstd', 'cXN%d' % j], w=['cXN%d' % j])
            to_fm(XN, np_, lambda ci: H2[:, ci, 0:ntok], 'cXN', 'H2_', scale_col=modcol[:, l, g, 2, :], bias_col=modcol[:, l, g, 3, :])
            H2res = ['H2_%d' % ci for ci in range(8)]
            rowlen = 64 if g == 0 else 256
            nrows = ntok // rowlen
            nts = [(0, 512), (512, 512)] if ntok == 1024 else [(0, 512)]
            for fh in range(2):
                for fi in range(11):
                    f = fh * 11 + fi
                    wslot = wu_i[0] % 3; wu_i[0] += 1
                    wu = WU[wslot]
                    P.op('sync', lambda e, f=f, wu=wu: e.dma_start(out=wu, in_=WUP[l, f]), r=['WUP%d_%d_0' % (l, f), 'WUP%d_%d_1' % (l, f)], w=['WU%d' % wslot],
                         dsem='ldWU%d' % wslot)
                    cbuf = f % 2
                    for h in range(2):
                        b0 = nextbank(2)
                        for ni, (n0, nn) in enumerate(nts):
                            for ci in range(8):
                                P.op('tensor', lambda e, ci=ci, h=h, n0=n0, nn=nn, b=b0 + ni, wu=wu: e.matmul(
                                    ps[:, b, 0:nn], lhsT=wu[:, ci, h, :], rhs=H2[:, ci, n0:n0 + nn], start=(ci == 0), stop=(ci == 7)),
                                    r=['WU%d' % wslot] + (H2res if ci == 0 else []), w=['ps%d' % (b0 + ni)])
                        pv = psb(b0, 2)[:, 0:ntok]
                        dst = (CV if h == 0 else CG)[cbuf][:, 0:ntok]
                        fcol = h * NF + f
                        dres = ('CV%d' if h == 0 else 'CG%d') % cbuf
                        pres = ['ps%d' % b0, 'ps%d' % (b0 + 1)]
                        P.op('scalar', lambda e, pv=pv, dst=dst, fcol=fcol: e.activation(
                            out=dst, in_=pv, func=AF.Identity, scale=fconv[:, l, 1, fcol:fcol + 1], bias=fconv[:, l, 3, fcol:fcol + 1]),
                            r=pres + ['fconv'], w=[dres])
                        pv3 = pv.rearrange("p (r n) -> p r n", n=rowlen)
                        d3 = dst.rearrange("p (r n) -> p r n", n=rowlen)
                        P.op('vector', lambda e, pv3=pv3, d3=d3, fcol=fcol: e.scalar_tensor_tensor(
                            out=d3[:, :, 1:rowlen], in0=pv3[:, :, 0:rowlen - 1], scalar=fconv[:, l, 0, fcol:fcol + 1], in1=d3[:, :, 1:rowlen],
                            op0=ALU.mult, op1=ALU.add), r=pres + ['fconv', dres], w=[dres])
                        P.op('vector', lambda e, pv3=pv3, d3=d3, fcol=fcol: e.scalar_tensor_tensor(
                            out=d3[:, :, 0:rowlen - 1], in0=pv3[:, :, 1:rowlen], scalar=fconv[:, l, 2, fcol:fcol + 1], in1=d3[:, :, 0:rowlen - 1],
                            op0=ALU.mult, op1=ALU.add), r=pres + ['fconv', dres], w=[dres])
                    cv = CV[cbuf][:, 0:ntok]; cg = CG[cbuf][:, 0:ntok]
                    P.op('scalar', lambda e, cg=cg: e.activation(out=cg, in_=cg, func=AF.Silu), r=['CG%d' % cbuf], w=['CG%d' % cbuf])
                    P.op('gpsimd', lambda e, cv=cv, cg=cg, fi=fi: e.tensor_tensor(out=ACTT[:, fi, 0:ntok], in0=cv, in1=cg, op=ALU.mult),
                         r=['CV%d' % cbuf, 'CG%d' % cbuf], w=[actres(fi)])
                for j in range(8):
                    b0 = nextbank(2)
                    for n in range(2):
                        for fi in range(11):
                            f = fh * 11 + fi
                            P.op('tensor', lambda e, fi=fi, f=f, j=j, n=n, b=b0 + n: e.matmul(
                                ps[0:np_, b, :], lhsT=ACTT[:, fi, j:ntok:8], rhs=WD[:, f, n * 512:(n + 1) * 512], start=(fi == 0), stop=(fi == 10)),
                                r=[actres(fi), 'WD'], w=['ps%d' % (b0 + n)])
                    pv = psb(b0, 2)
                    pres = ['ps%d' % b0, 'ps%d' % (b0 + 1)]
                    P.op('vector', lambda e, pv=pv, j=j: e.tensor_tensor(out=XT[0:np_, j, :], in0=XT[0:np_, j, :], in1=pv[0:np_, :], op=ALU.add),
                         r=pres + ['cXT%d' % j], w=['cXT%d' % j])

        def load_wd(l, np_):
            for half in range(2):
                P.op('gpsimd', lambda e, half=half: e.dma_start(out=WD[:, half * 11:(half + 1) * 11, :],
                                                               in_=ffn_w_down[l, half * 11 * 128:(half + 1) * 11 * 128, :].rearrange("(f p) n -> p f n", p=128)),
                     w=['WD'], dsem='ldWD')
            for half in range(2):
                P.op('vector', lambda e, half=half: e.tensor_tensor(out=WD[:, half * 11:(half + 1) * 11, :], in0=WD[:, half * 11:(half + 1) * 11, :],
                                                                  in1=GT2.unsqueeze(1).broadcast_to([128, 11, D]), op=ALU.mult),
                     r=['WD', 'GT2'], w=['WD'])

        def load_wmix(l):
            if l == 0:
                for half in range(2):
                    P.op('gpsimd', lambda e, half=half: e.dma_start(out=WMIX[:, half * 4:(half + 1) * 4, 0:D],
                                                                   in_=lru_w_out[half * 512:(half + 1) * 512, :].rearrange("(c p) n -> p c n", p=128)),
                         w=['WMIX'], dsem='ldWMIX')
            else:
                for q in range(4):
                    P.op('gpsimd', lambda e, q=q: e.dma_start(out=WMIX[:, q * 2:(q + 1) * 2, :],
                                                             in_=s5_w_glu[q * 256:(q + 1) * 256, :].rearrange("(c p) n -> p c n", p=128)),
                         w=['WMIX'], dsem='ldWMIX')
            P.op('vector', lambda e: e.tensor_tensor(out=WMIX[:, :, 0:D], in0=WMIX[:, :, 0:D], in1=GT1.unsqueeze(1).broadcast_to([128, 8, D]), op=ALU.mult),
                 r=['WMIX', 'GT1'], w=['WMIX'])

        XTres = ['cXT%d' % j for j in range(8)]
        JUNK1 = UGS[1].rearrange("p g k -> p (g k)").unsqueeze(1).broadcast_to([128, 8, 1024])
        if stop_after == 'C0':
            dump('Z0a', Z0, [8, 128, NT], BF16, r=Z0res)
            dump('T0', T01[0], [128, NT])
            dump('T1', T01[1], [128, NT])
            dump('GG', GG, [128, NT], BF16)
            dump('RA', RA, [128, NT])
            dump('XCb', XCb, [128, NT], BF16)
            dump('H', Hb, [128, 8, NT], BF16)
            dump('WIN1', WIN[1], [128, 8, 2, 128], BF16)
            P.op('sync', lambda e: e.dma_start(out=H2[:, :, 0:1024], in_=Z0[:, :, 0:1024].rearrange("c p n -> p c n")),
                 r=Z0res, w=['zt'], dsem='ldZT')
            dump('zt', H2, [128, 8, 1024], BF16, r=['zt'])
            dump('Z0b', Z0, [8, 128, NT], BF16, r=['zt'])
            for half in range(2):
                P.op('gpsimd', lambda e, half=half: e.dma_start(out=WD[:, half * 11:(half + 1) * 11, :],
                                                               in_=ffn_w_down[0, half * 11 * 128:(half + 1) * 11 * 128, :].rearrange("(f p) n -> p f n", p=128)),
                     w=['WD'], dsem='ldWD')
            dump('Z0c', Z0, [8, 128, NT], BF16, r=['WD'])
            return finish()
        ZT = H2

        def load_zt(t_):
            base_, np2 = TT[t_]
            nt_ = 8 * np2
            P.op('sync', lambda e: e.dma_start(out=ZT[:, :, 0:nt_], in_=Z0[:, :, base_:base_ + nt_].rearrange("c p n -> p c n")),
                 r=Z0res, w=['H2_%d' % ci for ci in range(8)], dsem='ldZT')
        for t in range(5):
            base, np_ = TT[t]
            g = 0 if t < 4 else 1
            ntok = 8 * np_
            if t == 0 or t == 4:
                load_bc(GT1, 0, g, 0, 128, 'GT1')
                load_bc(GT2, 0, g, 1, 128, 'GT2')
                load_wd(0, np_)
                load_wmix(0)
            P.op('sync', lambda e, t=t, np_=np_: e.dma_start(out=XT[0:np_], in_=xtile_dram(t)), w=XTres, dsem='ldXT')
            if t == 0:
                load_zt(0)
            for j in range(8):
                b0 = nextbank(2)
                for n in range(2):
                    for c in range(8):
                        P.op('tensor', lambda e, c=c, j=j, n=n, b=b0 + n, ntok=ntok, np_=np_: e.matmul(
                            ps[0:np_, b, :], lhsT=ZT[:, c, j:ntok:8], rhs=WMIX[:, c, n * 512:(n + 1) * 512], start=(c == 0), stop=(c == 7)),
                            r=['WMIX'] + (['H2_%d' % ci for ci in range(8)] if c == 0 else []), w=['ps%d' % (b0 + n)])
                pv = psb(b0, 2)
                pres = ['ps%d' % b0, 'ps%d' % (b0 + 1)]
                P.op('vector', lambda e, pv=pv, np_=np_, j=j: e.tensor_tensor(out=XT[0:np_, j, :], in0=XT[0:np_, j, :], in1=pv[0:np_, :], op=ALU.add),
                     r=pres + ['cXT%d' % j], w=['cXT%d' % j])
            if stop_after == 'C1' and t == 0:
                dump('x1', XT, [128, 8, D], r=XTres)
                dump('gt1', GT1, [128, D], r=['GT1'])
                dump('wmix', WMIX, [128, 8, 2048], BF16, r=['WMIX'])
                dump('zt', ZT, [128, 8, 1024], BF16, r=['H2_%d' % ci for ci in range(8)])
            ffn(0, t, np_, g)
            if stop_after == 'C1' and t == 0:
                dump('x2', XT, [128, 8, D], r=XTres)
                dump('h2', H2, [128, 8, 1024], BF16, r=['H2_%d' % ci for ci in range(8)])
                dump('actt', ACTT, [128, 11, 1024], BF16, r=[actres(fi) for fi in range(11)])
                dump('cv', CV[1], [128, 1024], r=['CV1'])
                dump('cg', CG[1], [128, 1024], r=['CG1'])
                dump('wd', WD, [128, NF, D], BF16, r=['WD'])
                dump('wu', WU[0], [128, 8, 2, 128], BF16, r=['WU0'])
                return finish()
            if t < 4:
                load_zt(t + 1)
            if stop_after not in ('C', 'C1'):
                if t == 0 or t == 4:
                    load_bc(BCA, 1, g, 2, np_, 'BCA')
                    load_bc(BCB, 1, g, 3, np_, 'BCB')
                rms_rstd(XT, np_, rstd, JUNK1, 'c', jres=lambda j: 'UGS1')
                Ugm = AXb[:, 0:8 * D].rearrange("p (g j c) -> p g j c", g=64, j=8)
                ugres = ['cXN%d' % j for j in range(8)]
                for j in range(8):
                    P.op('vector', lambda e, j=j, np_=np_: e.scalar_tensor_tensor(out=TMP[0:np_, :], in0=XT[0:np_, j, :], scalar=rstd[0:np_, j:j + 1],
                                                                             in1=BCA[0:np_, :], op0=ALU.mult, op1=ALU.mult),
                         r=['cXT%d' % j, 'rstd', 'BCA'], w=['TMP'])
                    P.op('vector', lambda e, j=j, np_=np_: e.tensor_tensor(out=Ugm[0:np_, :, j, :], in0=TMP[0:np_, :].rearrange("p (g c) -> p g c", g=64),
                                                                         in1=BCB[0:np_, :].rearrange("p (g c) -> p g c", g=64), op=ALU.add),
                         r=['TMP', 'BCB'], w=ugres)
                kb = t * 128 if t < 4 else 512
                for g8 in range(8):
                    b = nextbank()
                    for gi in range(8):
                        gg_ = g8 * 8 + gi
                        P.op('tensor', lambda e, gg_=gg_, gi=gi, b=b, np_=np_: e.transpose(
                            out=psbf(b)[:, gi * np_:(gi + 1) * np_], in_=Ugm[0:np_, gg_, :, :].rearrange("p j c -> p (j c)"),
                            identity=ident[0:np_, 0:np_]), r=ugres + ['ident'], w=['ps%d' % b])
                    us = UGS[g8 % 2]
                    P.op('scalar', lambda e, b=b, np_=np_, us=us: e.activation(
                        out=us[:, :, 0:np_], in_=psbf(b)[:, 0:8 * np_].rearrange("p (g k) -> p g k", g=8), func=AF.Copy),
                        r=['ps%d' % b], w=['UGS%d' % (g8 % 2)])
                    P.op('scalar', lambda e, g8=g8, kb=kb, np_=np_, us=us: e.dma_start(
                        out=UG[g8 * 8:(g8 + 1) * 8, :, kb:kb + np_].rearrange("g p k -> p g k"), in_=us[:, :, 0:np_]),
                        r=['UGS%d' % (g8 % 2)], w=['UGd'], dsem='stUG%d' % (g8 % 2))
                P.op('sync', lambda e, t=t, np_=np_: e.dma_start(out=xtile_dram(t, scr=True), in_=XT[0:np_]), r=XTres, w=['Xd%d' % t], dsem='stX')
            if stop_after == 'C':
                dst = (dbg['xs1'][base:base + 1024, :] if t < 4 else dbg['xp1'][:, :]).rearrange("(k j) c -> k j c", j=8)
                P.op('sync', lambda e, dst=dst, np_=np_: e.dma_start(out=dst, in_=XT[0:np_]), r=XTres, w=['dbgout'], dsem='stX')
        if stop_after in ('C', 'C1'):
            return finish()
        keep = {k: v for k, v in P.lastw.items() if k.startswith('WUP') or k.startswith('Xd') or k == 'UGd'}
        P.barrier()
        P.lastw.update(keep)

        TWO_PI = 6.283185307179586
        MAGIC = 12582912.0
        db = Bump(PBASE, NW)
        QT = db.bf(2 * 64 * 2 * 64).rearrange("p (d g r m) -> p d g r m", d=2, g=64, r=2)
        PT = db.bf(2 * 32 * 2 * 128).rearrange("p (d g r m) -> p d g r m", d=2, g=32, r=2)
        MT = db.bf(64 * 128).rearrange("p (g m) -> p g m", g=64)
        ARt = db.f32(128); AIt = db.f32(128)
        Hs = db.f32(128); Zs = db.f32(128); T1 = db.f32(128); T2 = db.f32(128)
        MASKF = db.f32(128); MASKB = db.f32(128); DCOL = db.f32(64)
        SEL = db.f32(128); TRI = db.f32(128)
        H0 = db.f32(128)
        H0b = db.bf(128)
        d_base = db.cur
        Qb = db.bf(2 * 32 * 2 * 128).rearrange("p (d g r m) -> p d g r m", d=2, g=32, r=2)
        BR, BI, CR, CI = [db.f32(2 * 32 * 16).rearrange("p (d g c) -> p d g c", d=2, g=32) for _ in range(4)]
        CN = db.f32(4 * 8 * 64).rearrange("p (a h w q) -> p a h w q", a=4, h=4, w=2)
        TA = db.f32(2048); TB = db.f32(2048)

        def t3():
            return db.f32(64).rearrange("p (d g) -> p d g", d=2)

        def t4():
            return db.f32(512).rearrange("p (d g e) -> p d g e", d=2, g=32)
        LDT = t3(); LR = t3(); TH = t3(); LAR = t3(); LAI = t3(); NR_ = t3(); L2 = t3(); BETR = t3(); BETI = t3(); TM3 = t3(); TM3b = t3()
        ANG = t4(); KK = t4(); SN = t4(); SH = t4(); CS = t4(); LRE = t4(); MG = t4(); MI = t4()
        WPr = t4(); WPi = t4(); WQr = t4(); WQi = t4(); TM4 = t4()

        def f3(a):
            return a.rearrange("p d g -> p (d g)")

        def f4(a):
            return a.rearrange("p d g e -> p (d g e)")

        def vop(fn, r, w, eng='vector'):
            P.op(eng, fn, r=r, w=w)

        G_BR = ['BR%d%d' % (d_, g_) for d_ in range(2) for g_ in range(2)]
        G_BI = ['BI%d%d' % (d_, g_) for d_ in range(2) for g_ in range(2)]
        G_LAR = ['LAR%d%d' % (d_, g_) for d_ in range(2) for g_ in range(2)]
        G_LAI = ['LAI%d%d' % (d_, g_) for d_ in range(2) for g_ in range(2)]
        G_LDT = ['LDT%d%d' % (d_, g_) for d_ in range(2) for g_ in range(2)]
        G_H0 = ['H0%s%d%d' % (x_, d_, g_) for x_ in 'ri' for d_ in range(2) for g_ in range(2)]
        G_CN = ['CN%d%d%d' % (d_, r__, g_) for d_ in range(2) for r__ in range(2) for g_ in range(2)]
        G_DCOL = ['DCOL%d' % j_ for j_ in range(8)]
        for d in range(2):
            for gh in range(2):
                ps_ = slice(gh * 64, (gh + 1) * 64)
                gs_ = slice(gh * 32, (gh + 1) * 32)
                P.op('sync', lambda e, d=d, ps_=ps_, gs_=gs_: e.dma_start(out=BR[ps_, d], in_=s5_b_re[d, gs_].rearrange("g p c -> p g c")), w=['BR%d%d' % (d, gh)], dsem='ldS5_BR')
                P.op('sync', lambda e, d=d, ps_=ps_, gs_=gs_: e.dma_start(out=BI[ps_, d], in_=s5_b_im[d, gs_].rearrange("g p c -> p g c")), w=['BI%d%d' % (d, gh)], dsem='ldS5_BI')
                P.op('sync', lambda e, d=d, ps_=ps_, gs_=gs_: e.dma_start(out=LAR[ps_, d], in_=s5_a_re[d, gs_].rearrange("g p -> p g")), w=['LAR%d%d' % (d, gh)], dsem='ldS5_LAR')
                P.op('sync', lambda e, d=d, ps_=ps_, gs_=gs_: e.dma_start(out=LAI[ps_, d], in_=s5_a_im[d, gs_].rearrange("g p -> p g")), w=['LAI%d%d' % (d, gh)], dsem='ldS5_LAI')
                P.op('sync', lambda e, d=d, ps_=ps_, gs_=gs_: e.dma_start(out=LDT[ps_, d], in_=s5_log_dt[d:d + 1, gs_].broadcast_to([64, 32])), w=['LDT%d%d' % (d, gh)], dsem='ldS5_LDT')
                P.op('sync', lambda e, d=d, ps_=ps_, gs_=gs_: e.dma_start(out=H0[ps_].rearrange("p (d r g) -> p d r g", d=2, r=2)[:, d, 0, :],
                                                                        in_=st_re[d, gs_].rearrange("g p -> p g")), w=['H0r%d%d' % (d, gh)], dsem='ldS5_H0')
                P.op('sync', lambda e, d=d, ps_=ps_, gs_=gs_: e.dma_start(out=H0[ps_].rearrange("p (d r g) -> p d r g", d=2, r=2)[:, d, 1, :],
                                                                        in_=st_im[d, gs_].rearrange("g p -> p g")), w=['H0i%d%d' % (d, gh)], dsem='ldS5_H0')
            for ri, csrc in enumerate([s5_c_re, s5_c_im]):
                for gh in range(2):
                    P.op('sync', lambda e, d=d, ri=ri, csrc=csrc, gh=gh: e.dma_start(
                        out=CN[:, d * 2 + ri, :, gh, :], in_=csrc[d, gh * 32:(gh + 1) * 32].rearrange("(h l) c p -> (l c) h p", l=8)),
                        w=['CN%d%d%d' % (d, ri, gh)], dsem='ldS5_CN')
        for j in range(8):
            P.op('sync', lambda e, j=j: e.dma_start(out=DCOL[j * 16:(j + 1) * 16, :], in_=s5_d[0].rearrange("(g c) -> c g", c=16)), w=['DCOL%d' % j], dsem='ldS5_DC')
        for d in range(2):
            for ri in range(2):
                b = nextbank()
                for h4 in range(4):
                    src = CN[:, d * 2 + ri, h4, :, :].rearrange("p w q -> p (w q)")
                    P.op('tensor', lambda e, src=src, b=b, h4=h4: e.transpose(out=ps[:, b, h4 * 128:(h4 + 1) * 128], in_=src,
                                                                            identity=identf), r=G_CN + ['identf'], w=['ps%d' % b])
                dstC = (CR if ri == 0 else CI)[:, d].rearrange("p g c -> p (g c)")
                P.op('vector', lambda e, dstC=dstC, b=b: e.tensor_copy(out=dstC, in_=ps[:, b, :]), r=['ps%d' % b], w=['CR' if ri == 0 else 'CI'])
        P.op('gpsimd', lambda e: e.memset(SEL[0:8, :], 1.0), w=['SEL'])
        P.op('gpsimd', lambda e: e.memset(TRI[0:8, :], 1.0), w=['TRI'])
        P.op('gpsimd', lambda e: e.affine_select(out=SEL[0:8, :], in_=SEL[0:8, :], pattern=[[1, 8], [0, 16]], compare_op=ALU.is_equal,
                                                 fill=0.0, base=0, channel_multiplier=-1), r=['SEL'], w=['SEL'])
        P.op('gpsimd', lambda e: e.affine_select(out=TRI[0:8, :], in_=TRI[0:8, :], pattern=[[1, 8], [0, 16]], compare_op=ALU.is_ge,
                                                 fill=0.0, base=0, channel_multiplier=-1), r=['TRI'], w=['TRI'])
        bm = nextbank()
        P.op('tensor', lambda e: e.matmul(ps[:, bm, 0:128], lhsT=SEL[0:8, :], rhs=TRI[0:8, :], start=True, stop=True), r=['SEL', 'TRI'], w=['ps%d' % bm])
        P.op('tensor', lambda e: e.matmul(ps[:, bm, 128:256], lhsT=TRI[0:8, :], rhs=SEL[0:8, :], start=True, stop=True), r=['SEL', 'TRI'], w=['ps%d' % bm])
        P.op('vector', lambda e: e.tensor_copy(out=MASKF, in_=ps[:, bm, 0:128]), r=['ps%d' % bm], w=['MASK'])
        P.op('vector', lambda e: e.tensor_copy(out=MASKB, in_=ps[:, bm, 128:256]), r=['ps%d' % bm], w=['MASK'])

        S = 'scalar'; V = 'vector'
        P.op(S, lambda e: e.activation(out=f3(LDT), in_=f3(LDT), func=AF.Exp), r=G_LDT, w=['LDTx'])
        vop(lambda e: e.tensor_tensor(out=f3(LR), in0=f3(LAR), in1=f3(LDT), op=ALU.mult), G_LAR + ['LDTx'], ['LR'])
        vop(lambda e: e.tensor_tensor(out=f3(TH), in0=f3(LAI), in1=f3(LDT), op=ALU.mult), G_LAI + ['LDTx'], ['TH'])
        for ei in range(8):
            vop(lambda e, ei=ei: e.tensor_scalar(out=ANG[:, :, :, ei], in0=TH, scalar1=float(ei + 1), scalar2=None, op0=ALU.mult), ['TH'], ['ANG'])
            vop(lambda e, ei=ei: e.tensor_scalar(out=LRE[:, :, :, ei], in0=LR, scalar1=float(ei + 1), scalar2=None, op0=ALU.mult), ['LR'], ['LRE'])
        vop(lambda e: e.tensor_scalar(out=f4(KK), in0=f4(ANG), scalar1=1.0 / TWO_PI, scalar2=MAGIC, op0=ALU.mult, op1=ALU.add), ['ANG'], ['KK'])
        vop(lambda e: e.tensor_scalar(out=f4(KK), in0=f4(KK), scalar1=-MAGIC, scalar2=None, op0=ALU.add), ['KK'], ['KK'])
        vop(lambda e: e.scalar_tensor_tensor(out=f4(ANG), in0=f4(KK), scalar=-TWO_PI, in1=f4(ANG), op0=ALU.mult, op1=ALU.add), ['KK', 'ANG'], ['ANG'])
        P.op(S, lambda e: e.activation(out=f4(SN), in_=f4(ANG), func=AF.Sin, scale=0.999999), r=['ANG'], w=['SN'])
        P.op(S, lambda e: e.activation(out=f4(SH), in_=f4(ANG), func=AF.Sin, scale=0.5), r=['ANG'], w=['SH'])
        vop(lambda e: e.tensor_tensor(out=f4(SH), in0=f4(SH), in1=f4(SH), op=ALU.mult), ['SH'], ['SH'])
        vop(lambda e: e.tensor_scalar(out=f4(CS), in0=f4(SH), scalar1=-2.0, scalar2=1.0, op0=ALU.mult, op1=ALU.add), ['SH'], ['CS'])
        P.op(S, lambda e: e.activation(out=f4(MG), in_=f4(LRE), func=AF.Exp), r=['LRE'], w=['MG'])
        P.op(S, lambda e: e.activation(out=f4(MI), in_=f4(LRE), func=AF.Exp, scale=-1.0), r=['LRE'], w=['MI'])
        vop(lambda e: e.tensor_tensor(out=f4(WPr), in0=f4(MG), in1=f4(CS), op=ALU.mult), ['MG', 'CS'], ['WPr'])
        vop(lambda e: e.tensor_tensor(out=f4(WPi), in0=f4(MG), in1=f4(SN), op=ALU.mult), ['MG', 'SN'], ['WPi'])
        vop(lambda e: e.tensor_scalar(out=NR_, in0=WPr[:, :, :, 0], scalar1=-1.0, scalar2=None, op0=ALU.add), ['WPr'], ['NR'])
        NI_ = WPi[:, :, :, 0]
        vop(lambda e: e.tensor_tensor(out=f3(L2), in0=f3(LAR), in1=f3(LAR), op=ALU.mult), G_LAR, ['L2'])
        vop(lambda e: e.tensor_tensor(out=f3(TM3), in0=f3(LAI), in1=f3(LAI), op=ALU.mult), G_LAI, ['TM3'])
        vop(lambda e: e.tensor_tensor(out=f3(L2), in0=f3(L2), in1=f3(TM3), op=ALU.add), ['L2', 'TM3'], ['L2'])
        vop(lambda e: e.reciprocal(out=f3(L2), in_=f3(L2)), ['L2'], ['L2'])
        vop(lambda e: e.tensor_tensor(out=BETR, in0=NR_, in1=LAR, op=ALU.mult), ['NR'] + G_LAR, ['BETR'])
        vop(lambda e: e.tensor_tensor(out=TM3, in0=NI_, in1=LAI, op=ALU.mult), ['WPi', 'L2'] + G_LAI, ['TM3'])
        vop(lambda e: e.tensor_tensor(out=f3(BETR), in0=f3(BETR), in1=f3(TM3), op=ALU.add), ['BETR', 'TM3'], ['BETR'])
        vop(lambda e: e.tensor_tensor(out=f3(BETR), in0=f3(BETR), in1=f3(L2), op=ALU.mult), ['BETR', 'L2'], ['BETR'])
        vop(lambda e: e.tensor_tensor(out=BETI, in0=NI_, in1=LAR, op=ALU.mult), ['WPi'] + G_LAR, ['BETI'])
        vop(lambda e: e.tensor_tensor(out=f3(TM3b), in0=f3(NR_), in1=f3(LAI), op=ALU.mult), ['NR'] + G_LAI, ['TM3b'])
        vop(lambda e: e.tensor_tensor(out=f3(BETI), in0=f3(BETI), in1=f3(TM3b), op=ALU.subtract), ['BETI', 'TM3b'], ['BETI'])
        vop(lambda e: e.tensor_tensor(out=f3(BETI), in0=f3(BETI), in1=f3(L2), op=ALU.mult), ['BETI', 'L2'], ['BETI'])
        def bc4(a):
            return a.unsqueeze(3).broadcast_to([128, 2, 32, 8])
        vop(lambda e: e.tensor_tensor(out=WQr, in0=CS, in1=bc4(BETR), op=ALU.mult), ['CS', 'BETR'], ['WQr'])
        vop(lambda e: e.tensor_tensor(out=TM4, in0=SN, in1=bc4(BETI), op=ALU.mult), ['SN', 'BETI'], ['TM4'])
        vop(lambda e: e.tensor_tensor(out=f4(WQr), in0=f4(WQr), in1=f4(TM4), op=ALU.add), ['WQr', 'TM4'], ['WQr'])
        vop(lambda e: e.tensor_tensor(out=f4(WQr), in0=f4(WQr), in1=f4(MI), op=ALU.mult), ['WQr', 'MI'], ['WQr'])
        vop(lambda e: e.tensor_tensor(out=WQi, in0=CS, in1=bc4(BETI), op=ALU.mult), ['CS', 'BETI'], ['WQi'])
        vop(lambda e: e.tensor_tensor(out=TM4, in0=SN, in1=bc4(BETR), op=ALU.mult), ['SN', 'BETR', 'WQr'], ['TM4'])
        vop(lambda e: e.tensor_tensor(out=f4(WQi), in0=f4(WQi), in1=f4(TM4), op=ALU.subtract), ['WQi', 'TM4'], ['WQi'])
        vop(lambda e: e.tensor_tensor(out=f4(WQi), in0=f4(WQi), in1=f4(MI), op=ALU.mult), ['WQi', 'MI'], ['WQi'])
        ARv = ARt.rearrange("p (d r g) -> p d r g", d=2, r=2)
        AIv = AIt.rearrange("p (d r g) -> p d r g", d=2, r=2)
        for r_ in range(2):
            vop(lambda e, r_=r_: e.tensor_copy(out=ARv[:, :, r_, :], in_=WPr[:, :, :, 7]), ['WPr'], ['ARt'])
        vop(lambda e: e.tensor_scalar(out=AIv[:, :, 0, :], in0=WPi[:, :, :, 7], scalar1=-1.0, scalar2=None, op0=ALU.mult), ['WPi'], ['AIt'])
        vop(lambda e: e.tensor_copy(out=AIv[:, :, 1, :], in_=WPi[:, :, :, 7]), ['WPi'], ['AIt'])

        def jord(W, d, gq):
            w_ = W[:, d, gq * 16:(gq + 1) * 16]
            if d == 1:
                w_ = w_[:, :, ::-1]
            return w_.unsqueeze(3).broadcast_to([128, 16, 8, 16])

        def bcj(X, d, gq):
            return X[:, d, gq * 16:(gq + 1) * 16].unsqueeze(2).broadcast_to([128, 16, 8, 16])
        TA4 = TA.rearrange("p (g j c) -> p g j c", g=16, j=8); TB4 = TB.rearrange("p (g j c) -> p g j c", g=16, j=8)

        def cplx_table(dst, d, Xr, Xi, Wr, Wi, xres, negim):
            for gq in range(2):
                gsl_ = slice(gq * 16, (gq + 1) * 16)
                o0 = dst[:, d, gsl_, 0, :].rearrange("p g (j c) -> p g j c", j=8)
                o1 = dst[:, d, gsl_, 1, :].rearrange("p g (j c) -> p g j c", j=8)
                P.op('vector', lambda e, gq=gq: e.tensor_tensor(out=TA4, in0=bcj(Xr, d, gq), in1=jord(Wr, d, gq), op=ALU.mult), r=xres + ['tbl'], w=['TA'])
                P.op('gpsimd', lambda e, gq=gq: e.tensor_tensor(out=TB4, in0=bcj(Xi, d, gq), in1=jord(Wi, d, gq), op=ALU.mult), r=xres + ['tbl'], w=['TB'])
                P.op('vector', lambda e, o0=o0: e.tensor_tensor(out=o0, in0=TA4, in1=TB4, op=ALU.subtract), r=['TA', 'TB'], w=['tbl'])
                P.op('vector', lambda e, gq=gq: e.tensor_tensor(out=TA4, in0=bcj(Xr, d, gq), in1=jord(Wi, d, gq), op=ALU.mult), r=xres + ['tbl'], w=['TA'])
                P.op('gpsimd', lambda e, gq=gq: e.tensor_tensor(out=TB4, in0=bcj(Xi, d, gq), in1=jord(Wr, d, gq), op=ALU.mult), r=xres + ['tbl'], w=['TB'])
                if negim:
                    P.op('vector', lambda e, o1=o1: e.scalar_tensor_tensor(out=o1, in0=TA4, scalar=-1.0, in1=TB4, op0=ALU.mult, op1=ALU.subtract),
                         r=['TA', 'TB'], w=['tbl'])
                else:
                    P.op('vector', lambda e, o1=o1: e.tensor_tensor(out=o1, in0=TA4, in1=TB4, op=ALU.add), r=['TA', 'TB'], w=['tbl'])
        for d in range(2):
            cplx_table(Qb, d, BR, BI, WQr, WQi, G_BR + G_BI + ['WQr', 'WQi'], False)
            cplx_table(PT, d, CR, CI, WPr, WPi, ['CR', 'CI', 'WPr', 'WPi'], True)

        TA3 = TA[:, 0:512].rearrange("p (g m) -> p g m", g=4); TB3 = TB[:, 0:512].rearrange("p (g m) -> p g m", g=4)
        for g4 in range(16):
            bf_ = nextbank(); bb_ = nextbank()
            for gi in range(4):
                g = g4 * 4 + gi
                gh, g32 = g // 32, g % 32
                psl = slice(gh * 64, (gh + 1) * 64)
                for d, bk in ((0, bf_), (1, bb_)):
                    for r_ in range(2):
                        P.op('tensor', lambda e, d=d, bk=bk, r_=r_, gi=gi, g32=g32, psl=psl: e.matmul(
                            ps[:, bk, gi * 128:(gi + 1) * 128], lhsT=Qb[psl, d, g32, r_, :], rhs=PT[psl, d, g32, r_, :], start=(r_ == 0), stop=(r_ == 1)),
                            r=['tbl'], w=['ps%d' % bk])
            gsl = slice(g4 * 4, g4 * 4 + 4)
            mf = MASKF.unsqueeze(1).broadcast_to([128, 4, 128]); mb = MASKB.unsqueeze(1).broadcast_to([128, 4, 128])
            idb = identf.unsqueeze(1).broadcast_to([128, 4, 128])
            dcb = DCOL[:, gsl].unsqueeze(2).broadcast_to([128, 4, 128])
            P.op('vector', lambda e, bf_=bf_, mf=mf: e.tensor_tensor(out=TA3, in0=ps[:, bf_, :].rearrange("p (g m) -> p g m", g=4), in1=mf, op=ALU.mult),
                 r=['ps%d' % bf_, 'MASK'], w=['TA'])
            P.op('vector', lambda e, bb_=bb_, mb=mb: e.tensor_tensor(out=TB3, in0=ps[:, bb_, :].rearrange("p (g m) -> p g m", g=4), in1=mb, op=ALU.mult),
                 r=['ps%d' % bb_, 'MASK'], w=['TB'])
            P.op('gpsimd', lambda e: e.tensor_tensor(out=TA3, in0=TA3, in1=TB3, op=ALU.add), r=['TA', 'TB'], w=['TA'])
            P.op('gpsimd', lambda e, idb=idb, dcb=dcb: e.tensor_tensor(out=TB3, in0=idb, in1=dcb, op=ALU.mult), r=['TA', 'identf'] + G_DCOL, w=['TB'])
            P.op('vector', lambda e, gsl=gsl: e.tensor_tensor(out=MT[:, gsl, :], in0=TA3, in1=TB3, op=ALU.add), r=['TA', 'TB'], w=['MT'])
        for d in range(2):
            for g8 in range(8):
                b = nextbank()
                for gi in range(8):
                    g = g8 * 8 + gi
                    gh, g32 = g // 32, g % 32
                    psl = slice(gh * 64, (gh + 1) * 64)
                    for r_ in range(2):
                        P.op('tensor', lambda e, d=d, b=b, gi=gi, r_=r_, g32=g32, psl=psl: e.transpose(
                            out=psbf(b)[:, (gi * 2 + r_) * 64:(gi * 2 + r_ + 1) * 64], in_=Qb[psl, d, g32, r_, :], identity=ident[psl, psl]),
                            r=['tbl', 'ident'], w=['ps%d' % b])
                P.op('scalar', lambda e, d=d, g8=g8, b=b: e.activation(out=QT[:, d, g8 * 8:(g8 + 1) * 8].rearrange("p g r m -> p (g r m)"),
                                                                      in_=psbf(b), func=AF.Copy), r=['ps%d' % b], w=['QT'])
        P.barrier()

        sc = Bump(d_base, NW)
        KC = 64
        SB_ = sc.f32(2 * 2 * 32 * KC).rearrange("p (d r g k) -> p d r g k", d=2, r=2, g=32)
        HBF = sc.f32(2 * 2 * 32 * KC).rearrange("p (d r g k) -> p d r g k", d=2, r=2, g=32)
        HSB = sc.bf(2 * 2 * 32 * KC).rearrange("p (d r g k) -> p d r g k", d=2, r=2, g=32)
        UGF = [sc.bf(64 * KC).rearrange("p (g k) -> p g k", g=64) for _ in range(2)]
        ZER = sc.bf(128)
        P.op('gpsimd', lambda e: e.memset(ZER, 0.0), w=['ZER'])
        Zsw = Zs.rearrange("p (d r g) -> p d r g", d=2, r=2)[:, :, ::-1, :]
        T2v = T2.rearrange("p (d r g) -> p d r g", d=2, r=2)
        AIv4 = AIt.rearrange("p (d r g) -> p d r g", d=2, r=2)
        TT4 = sc.f32(256).rearrange("p (d t s g) -> p d t s g", d=2, t=2, s=2)
        WW4 = sc.f32(256).rearrange("p (d t s g) -> p d t s g", d=2, t=2, s=2)
        ARv4 = ARt.rearrange("p (d r g) -> p d r g", d=2, r=2)
        P.op('vector', lambda e: e.tensor_copy(out=WW4[:, :, 0], in_=ARv4), r=['ARt'], w=['WW'])
        P.op('vector', lambda e: e.tensor_copy(out=WW4[:, :, 1], in_=AIv4[:, :, ::-1, :]), r=['AIt'], w=['WW'])
        SEQC = [(0, 512, 0), (512, 32, 513), (544, 32, 546)]
        for si, (c0, n, hb) in enumerate(SEQC):
            kc = min(KC, n)
            nb = n // kc
            if si == 0:
                P.op('vector', lambda e: e.tensor_copy(out=Hs, in_=H0), r=G_H0, w=['Hs'])
                P.op('gpsimd', lambda e: e.tensor_copy(out=H0b, in_=H0), r=G_H0, w=['H0b'])
                h0src = H0b
                h0res = 'H0b'
            else:
                P.op('vector', lambda e: e.memset(Hs, 0.0), w=['Hs'])
                h0src = ZER
                h0res = 'ZER'
            h0v = h0src.rearrange("p (d q) -> p d q", d=2)
            P.op('sync', lambda e, hb=hb, h0v=h0v: e.dma_start(out=HSD[0, :, :, hb:hb + 1], in_=h0v[:, 0, :].unsqueeze(2)), r=[h0res], w=['HSD'], dsem='stH0')
            P.op('sync', lambda e, hb=hb, n=n, h0v=h0v: e.dma_start(out=HSD[1, :, :, hb + n:hb + n + 1], in_=h0v[:, 1, :].unsqueeze(2)), r=[h0res], w=['HSD'], dsem='stH0')
            for bi in range(nb):
                kf = c0 + bi * kc
                kbk = c0 + (nb - 1 - bi) * kc
                P.op('sync', lambda e, kf=kf, kc=kc: e.dma_start(out=UGF[0][:, :, 0:kc], in_=UG[:, :, kf:kf + kc].rearrange("g p k -> p g k")),
                     r=['UGd'], w=['UGF0'], dsem='ldUGF0')
                P.op('sync', lambda e, kbk=kbk, kc=kc: e.dma_start(out=UGF[1][:, :, 0:kc], in_=UG[:, :, kbk:kbk + kc].rearrange("g p k -> p g k")),
                     r=['UGd'], w=['UGF1'], dsem='ldUGF1')
                for d in range(2):
                    for r_ in range(2):
                        for q in range(4):
                            b = nextbank()
                            for g8 in range(8):
                                g32 = q * 8 + g8
                                for gh in range(2):
                                    g = gh * 32 + g32
                                    P.op('tensor', lambda e, d=d, r_=r_, g=g, gh=gh, g8=g8, b=b, kc=kc: e.matmul(
                                        ps[gh * 64:(gh + 1) * 64, b, g8 * kc:(g8 + 1) * kc], lhsT=QT[:, d, g, r_, :], rhs=UGF[d][:, g, 0:kc],
                                        start=True, stop=True), r=['QT', 'UGF%d' % d], w=['ps%d' % b])
                            src = ps[:, b, 0:8 * kc].rearrange("p (g k) -> p g k", g=8)
                            dst = SB_[:, d, r_, q * 8:(q + 1) * 8, 0:kc]
                            if d == 1:
                                dst = dst[:, :, ::-1]
                            P.op('scalar', lambda e, src=src, dst=dst: e.activation(out=dst, in_=src, func=AF.Copy), r=['ps%d' % b], w=['SB'])
                def hd(x, dd):
                    return x.rearrange("p (d q) -> p d q", d=2)[:, dd, :]
                for kq in range(kc):
                    first = (kq == 0)
                    for dd in range(2):
                        sbk = SB_[:, dd, :, :, kq].rearrange("p r g -> p (r g)")
                        if kq == 0 and bi == 0:
                            hprev = hd(Hs, dd)
                        elif kq == 0:
                            hprev = HBF[:, dd, :, :, kc - 1].rearrange("p r g -> p (r g)")
                        else:
                            hprev = HBF[:, dd, :, :, kq - 1].rearrange("p r g -> p (r g)")
                        P.op('vector', lambda e, sbk=sbk, hprev=hprev, dd=dd: e.tensor_tensor(out=hd(Zs, dd), in0=hprev, in1=sbk, op=ALU.add),
                             r=['Hs', 'SB', 'HBF'], w=['Zs'], skip_self=not first)
                    for dd in range(2):
                        zb = hd(Zs, dd).rearrange("p (r g) -> p r g", r=2).unsqueeze(1).broadcast_to([128, 2, 2, 32])
                        P.op('vector', lambda e, dd=dd, zb=zb: e.tensor_tensor(out=TT4[:, dd], in0=zb, in1=WW4[:, dd], op=ALU.mult),
                             r=['Zs', 'WW'], w=['TT'], skip_self=not first)
                    for dd in range(2):
                        hout = HBF[:, dd, :, :, kq]
                        P.op('vector', lambda e, hout=hout, dd=dd: e.tensor_tensor(out=hout, in0=TT4[:, dd, 0], in1=TT4[:, dd, 1][:, ::-1, :], op=ALU.add),
                             r=['TT'] + (['HSB'] if kq == 0 else []), w=['HBF'], skip_self=not first)
                P.op('scalar', lambda e, kc=kc: e.activation(out=HSB[:, 0, :, :, 0:kc], in_=HBF[:, 0, :, :, 0:kc], func=AF.Copy), r=['HBF'], w=['HSB'])
                P.op('scalar', lambda e, kc=kc: e.activation(out=HSB[:, 1, :, :, 0:kc][:, :, :, ::-1], in_=HBF[:, 1, :, :, 0:kc], func=AF.Copy),
                     r=['HBF'], w=['HSB'])
                P.op('sync', lambda e, hb=hb, kf=kf, c0=c0, kc=kc: e.dma_start(
                    out=HSD[0, :, :, hb + (kf - c0) + 1: hb + (kf - c0) + 1 + kc], in_=HSB[:, 0, :, :, 0:kc].rearrange("p r g k -> p (r g) k")),
                    r=['HSB'], w=['HSD'], dsem='stHS')
                P.op('sync', lambda e, hb=hb, kbk=kbk, c0=c0, kc=kc: e.dma_start(
                    out=HSD[1, :, :, hb + (kbk - c0): hb + (kbk - c0) + kc], in_=HSB[:, 1, :, :, 0:kc].rearrange("p r g k -> p (r g) k")),
                    r=['HSB'], w=['HSD'], dsem='stHS')
            if si > 0:
                hv4 = HBF[:, :, :, :, kc - 1]
                for gh in range(2):
                    for ri, dstt in enumerate([ns5re, ns5im]):
                        for d in range(2):
                            P.op('sync', lambda e, gh=gh, ri=ri, dstt=dstt, si=si, hv4=hv4, d=d: e.dma_start(
                                out=dstt[si - 1, d, gh * 32:(gh + 1) * 32, :].rearrange("g p -> p g"), in_=hv4[gh * 64:(gh + 1) * 64, d, ri, :]),
                                r=['HBF'], w=['ns5'], dsem='stout')
        P.barrier()

        yb = Bump(d_base, NW)
        KY = 256
        UGY = yb.bf(64 * KY).rearrange("p (g k) -> p g k", g=64)
        HSF = yb.bf(64 * KY).rearrange("p (r g k) -> p r g k", r=2, g=32)
        HSG = yb.bf(64 * KY).rearrange("p (r g k) -> p r g k", r=2, g=32)
        for si, (c0, n, hb) in enumerate(SEQC):
            kc = min(KY, n)
            gpb = 512 // kc if kc >= 64 else 8
            gpb = min(gpb, 8)
            for bi in range(n // kc):
                k0 = c0 + bi * kc
                col = hb + bi * kc
                P.op('sync', lambda e, k0=k0, kc=kc: e.dma_start(out=UGY[:, :, 0:kc], in_=UG[:, :, k0:k0 + kc].rearrange("g p k -> p g k")),
                     r=['UGd'], w=['UGY%d' % q for q in range(64)], dsem='ldUGY')
                P.op('sync', lambda e, col=col, kc=kc: e.dma_start(out=HSF[:, :, :, 0:kc].rearrange("p r g k -> p (r g) k"), in_=HSD[0, :, :, col:col + kc]),
                     w=['HSF'], dsem='ldHSF')
                P.op('sync', lambda e, col=col, kc=kc: e.dma_start(out=HSG[:, :, :, 0:kc].rearrange("p r g k -> p (r g) k"), in_=HSD[1, :, :, col + 1:col + 1 + kc]),
                     w=['HSG'], dsem='ldHSG')
                for gb in range(64 // gpb):
                    b = nextbank()
                    gl = [gb * gpb + gi for gi in range(gpb)]
                    ures = ['UGY%d' % g for g in gl]
                    for gi, g in enumerate(gl):
                        o = ps[:, b, gi * kc:(gi + 1) * kc]
                        P.op('tensor', lambda e, o=o, g=g, kc=kc, gi=gi: e.matmul(o, lhsT=MT[:, g, :], rhs=UGY[:, g, 0:kc], start=(gi == 0), stop=False),
                             r=ures + ['HSF', 'HSG'], w=['ps%d' % b])
                    for gi, g in enumerate(gl):
                        gh, g32 = g // 32, g % 32
                        psl = slice(gh * 64, (gh + 1) * 64)
                        o = ps[:, b, gi * kc:(gi + 1) * kc]
                        for d, hsrc in ((0, HSF), (1, HSG)):
                            for r_ in range(2):
                                last = (gi == gpb - 1 and d == 1 and r_ == 1)
                                P.op('tensor', lambda e, o=o, d=d, r_=r_, g32=g32, psl=psl, hsrc=hsrc, kc=kc, last=last: e.matmul(
                                    o, lhsT=PT[psl, d, g32, r_, :], rhs=hsrc[psl, r_, g32, 0:kc], start=False, stop=last),
                                    w=['ps%d' % b])
                    P.op('scalar', lambda e, b=b, gl=gl, kc=kc, gpb=gpb: e.activation(
                        out=UGY[:, gl[0]:gl[0] + gpb, 0:kc], in_=ps[:, b, 0:gpb * kc].rearrange("p (g k) -> p g k", g=gpb), func=AF.Gelu_apprx_tanh),
                        r=['ps%d' % b], w=ures)
                P.op('sync', lambda e, k0=k0, kc=kc: e.dma_start(out=UG[:, :, k0:k0 + kc].rearrange("g p k -> p g k"), in_=UGY[:, :, 0:kc]),
                     r=['UGY%d' % q for q in range(64)], w=['UGz'], dsem='stUGz')
        keep = {k: v for k, v in P.lastw.items() if k.startswith('WUP') or k.startswith('Xd') or k == 'UGz'}
        P.barrier()
        P.lastw.update(keep)

        ZG = WU[0]
        ZGb = CV[0]
        ZGT = TMP
        SGm = BCB
        Ztm = AXb[:, 0:8 * D].rearrange("p (j c) -> p j c", j=8)
        ZGL = H2
        def load_zgl(t_):
            _, np2 = TT[t_]
            kb_ = t_ * 128 if t_ < 4 else 512
            zg_ = H2.rearrange("p c n -> p (c n)")[:, 0:64 * np2].rearrange("p (g k) -> p g k", g=64)
            P.op('sync', lambda e: e.dma_start(out=zg_, in_=UG[:, :, kb_:kb_ + np2].rearrange("g p k -> p g k")),
                 r=['UGz'], w=['H2_%d' % ci for ci in range(8)], dsem='ldZT')

        for t in range(5):
            base, np_ = TT[t]
            g = 0 if t < 4 else 1
            ntok = 8 * np_
            kb = t * 128 if t < 4 else 512
            if t == 0 or t == 4:
                load_bc(GT1, 1, g, 0, 128, 'GT1')
                load_bc(GT2, 1, g, 1, 128, 'GT2')
                load_wd(1, np_)
                load_wmix(1)
                if t == 0:
                    P.op('sync', lambda e: e.dma_start(out=BCA, in_=norm_final[0:1, :].broadcast_to([128, D])), w=['BCA'], dsem='ldBCA')
            P.op('sync', lambda e, t=t, np_=np_: e.dma_start(out=XT[0:np_], in_=xtile_dram(t, scr=True)), r=['Xd%d' % t], w=XTres, dsem='ldXT')
            zgl = H2.rearrange("p c n -> p (c n)")[:, 0:64 * np_].rearrange("p (g k) -> p g k", g=64)
            if t == 0:
                load_zgl(0)
            for g8 in range(8):
                b = nextbank()
                for gi in range(8):
                    P.op('tensor', lambda e, g8=g8, gi=gi, b=b, np_=np_, zgl=zgl: e.transpose(
                        out=psbf(b)[0:np_, gi * 128:(gi + 1) * 128], in_=zgl[:, g8 * 8 + gi, :], identity=ident),
                        r=['H2_%d' % ci for ci in range(8)] + ['ident'], w=['ps%d' % b])
                srcv = psbf(b)[0:np_, :].rearrange("p (g j c) -> p g j c", g=8, j=8)
                dstv = Ztm[0:np_, :, g8 * 128:(g8 + 1) * 128].rearrange("p j (g c) -> p g j c", g=8)
                P.op('scalar', lambda e, srcv=srcv, dstv=dstv: e.activation(out=dstv, in_=srcv, func=AF.Copy),
                     r=['ps%d' % b], w=['cXN%d' % j for j in range(8)])
            to_fm(Ztm, np_, lambda ci, ntok=ntok: H2[:, ci, 0:ntok], 'cXN', 'H2_')
            H2res = ['H2_%d' % ci for ci in range(8)]
            for j in range(8):
                b0 = nextbank(4)
                for n in range(4):
                    for ci in range(8):
                        P.op('tensor', lambda e, ci=ci, j=j, n=n, b=b0 + n, ntok=ntok, np_=np_: e.matmul(
                            ps[0:np_, b, :], lhsT=H2[:, ci, j:ntok:8], rhs=WMIX[:, ci, n * 512:(n + 1) * 512], start=(ci == 0), stop=(ci == 7)),
                            r=['WMIX'] + (H2res if ci == 0 else []), w=['ps%d' % (b0 + n)])
                pvv = psb(b0, 2); pvg = psb(b0 + 2, 2)
                P.op('scalar', lambda e, pvg=pvg, np_=np_: e.activation(out=SGm[0:np_, :], in_=pvg[0:np_, :], func=AF.Sigmoid),
                     r=['ps%d' % (b0 + 2), 'ps%d' % (b0 + 3)], w=['SGm'])
                P.op('vector', lambda e, pvv=pvv, np_=np_: e.tensor_tensor(out=TMP[0:np_, :], in0=pvv[0:np_, :], in1=SGm[0:np_, :], op=ALU.mult),
                     r=['ps%d' % b0, 'ps%d' % (b0 + 1), 'SGm'], w=['TMP'])
                P.op('vector', lambda e, j=j, np_=np_: e.tensor_tensor(out=XT[0:np_, j, :], in0=XT[0:np_, j, :], in1=TMP[0:np_, :], op=ALU.add),
                     r=['TMP', 'cXT%d' % j], w=['cXT%d' % j])
            ffn(1, t, np_, g)
            if t < 4:
                load_zgl(t + 1)
            rms_rstd(XT, np_, rstd, JUNK1, 'c', jres=lambda j: 'UGS1')
            for j in range(8):
                P.op('vector', lambda e, j=j, np_=np_: e.scalar_tensor_tensor(out=XT[0:np_, j, :], in0=XT[0:np_, j, :], scalar=rstd[0:np_, j:j + 1],
                                                                         in1=BCA[0:np_, :], op0=ALU.mult, op1=ALU.mult),
                     r=['cXT%d' % j, 'rstd', 'BCA'], w=['cXT%d' % j])
            P.op('sync', lambda e, t=t, np_=np_: e.dma_start(out=xtile_dram(t, for_out=True), in_=XT[0:np_]), r=XTres, w=['Xd%d' % t], dsem='stX')

        return finish()


def _in_maps(inp):
    f = lambda a: np.ascontiguousarray(np.asarray(a, dtype=np.float32))
    maps = []
    for b in range(8):
        m = {
            "xs": f(inp["x_sample"][b]),
            "xp": f(inp["x_prompt"][2 * b:2 * b + 2].reshape(2 * LP, D)),
            "st_lru": f(inp["state_lru"][b, 0]),
            "st_re": f(inp["state_s5_re"][b, 0]),
            "st_im": f(inp["state_s5_im"][b, 0]),
            "cvec": f(np.stack([np.asarray(inp["c"])[b], np.asarray(inp["c_ctx"])], 0)),
            "ada_w": f(inp["ada_w"]), "ada_b": f(inp["ada_b"]),
            "norm_mix": f(inp["norm_mix"]), "norm_ffn": f(inp["norm_ffn"]), "norm_final": f(np.asarray(inp["norm_final"])[None, :]),
            "lru_w_in": f(inp["lru_w_in"][0]), "lru_conv_w": f(inp["lru_conv_w"][0]), "lru_conv_b": f(np.asarray(inp["lru_conv_b"])[0][None, :]),
            "lru_w_a": f(inp["lru_w_a"][0]), "lru_b_a": f(inp["lru_b_a"][0]),
            "lru_w_i": f(inp["lru_w_i"][0]), "lru_b_i": f(inp["lru_b_i"][0]),
            "lru_lambda": f(inp["lru_lambda"][0]), "lru_w_out": f(inp["lru_w_out"][0]),
            "s5_a_re": f(inp["s5_a_re"][0]), "s5_a_im": f(inp["s5_a_im"][0]), "s5_log_dt": f(inp["s5_log_dt"][0]),
            "s5_b_re": f(inp["s5_b_re"][0]), "s5_b_im": f(inp["s5_b_im"][0]),
            "s5_c_re": f(inp["s5_c_re"][0]), "s5_c_im": f(inp["s5_c_im"][0]),
            "s5_d": f(np.asarray(inp["s5_d"])[0][None, :]), "s5_w_glu": f(inp["s5_w_glu"][0]),
            "ffn_w_up": f(inp["ffn_w_up"]), "ffn_conv_w": f(inp["ffn_conv_w"]), "ffn_conv_b": f(inp["ffn_conv_b"]),
            "ffn_w_down": f(inp["ffn_w_down"]),
        }
        maps.append(m)
    return maps


def run(inp, stop_after='E', trace=False):
    nc = build(stop_after)
    res = run_bass_kernel_spmd(nc, _in_maps(inp), core_ids=list(range(8)), trace=trace)
    return res


def kernel(**inp):
    res = run(inp)
    r = res.results
    y_prompt = np.concatenate([r[b]["yp"].reshape(2, LP, D) for b in range(8)], 0)
    y_sample = np.stack([r[b]["ys"] for b in range(8)], 0)
    nl = np.concatenate([r[b]["nlru"].reshape(2, 1, 2, D) for b in range(8)], 0)
    nre = np.concatenate([r[b]["ns5re"].reshape(2, 1, 2, 64, 64) for b in range(8)], 0)
    nim = np.concatenate([r[b]["ns5im"].reshape(2, 1, 2, 64, 64) for b in range(8)], 0)
    return (y_prompt.astype(np.float32), y_sample.astype(np.float32), nl.astype(np.float32),
            nre.astype(np.float32), nim.astype(np.float32))
```
